# Optimizing a Trainium2 kernel written in Bass

```python
import jax, jax.numpy as jnp
from jax import lax
import numpy as np

D_MODEL = 1024
BATCH = 8
SEQ = 2048
DEPTH = 2

GRID_W = 64
D_MIX = D_MODEL
HG_DIM = D_MIX // 2
HG_EXPAND = 128
HG_HEADS = HG_DIM // HG_EXPAND
HG_HEAD_DIM = HG_DIM // HG_HEADS
HG_CHUNK = 32
NA_DIM = D_MIX // 4
NA_HEAD_DIM = 64
NA_HEADS = NA_DIM // NA_HEAD_DIM
NA_KH = 8
NA_KW = 16
NA_QB_W = 16
NA_KB_W = NA_QB_W + NA_KW
NA_NCB = GRID_W // NA_QB_W
CONV_DIM = D_MIX - HG_DIM - NA_DIM
CONV_WIDTH = 31
IN_SPLITS = (HG_DIM, HG_DIM, HG_DIM, HG_DIM, HG_DIM,
             NA_DIM, NA_DIM, NA_DIM,
             CONV_DIM, CONV_DIM)
D_IN = sum(IN_SPLITS)
FFN_DIM = -(-8 * D_MODEL // (3 * 256)) * 256
EPS = 1e-6
NEG_INF = -1e30

kernel_name = "hymba_style_hgrn2_natten_conformer_encoder"


def _rmsnorm(x, g):
    xf = x.astype(jnp.float32)
    y = xf * lax.rsqrt(jnp.mean(xf * xf, axis=-1, keepdims=True) + EPS)
    return (y * g.astype(jnp.float32)).astype(x.dtype)


def _chunk_gated_recurrence(q, k, v, g):
    B, H, T, dk = q.shape
    dv = v.shape[-1]
    n = T // HG_CHUNK
    cs = lambda t: t.reshape(B, H, n, HG_CHUNK, t.shape[-1])
    q, k, v, g = cs(q), cs(k), cs(v), cs(g)
    b = jnp.cumsum(g, axis=3)
    q_in = q * jnp.exp(b)
    k_in = k * jnp.exp(-b)
    scores = jnp.einsum('bhncd,bhnsd->bhncs', q_in, k_in)
    mask = np.tril(np.ones((HG_CHUNK, HG_CHUNK), dtype=bool))
    scores = jnp.where(mask, scores, 0.0)
    o_intra = jnp.einsum('bhncs,bhnsv->bhncv', scores, v)
    b_last = b[:, :, :, -1:, :]
    d_state = jnp.einsum('bhncd,bhncv->bhndv', k * jnp.exp(b_last - b), v)
    decay = jnp.exp(b_last[:, :, :, 0, :])

    def step(S, inp):
        ds_n, dec_n, q_n = inp
        o_n = jnp.einsum('bhcd,bhdv->bhcv', q_n, S)
        return dec_n[..., None] * S + ds_n, o_n

    S0 = jnp.zeros((B, H, dk, dv), jnp.float32)
    _, o_inter = lax.scan(step, S0, (jnp.moveaxis(d_state, 2, 0), jnp.moveaxis(decay, 2, 0),
                                     jnp.moveaxis(q_in, 2, 0)))
    o = o_intra + jnp.moveaxis(o_inter, 0, 2)
    return o.reshape(B, H, T, dv)


def _hgrn2(q_raw, f_fwd_raw, f_bwd_raw, i_raw, g_raw, lb, norm_g):
    B, T, _ = q_raw.shape
    heads = lambda t: t.reshape(B, T, HG_HEADS, -1).transpose(0, 2, 1, 3)
    q = heads(jax.nn.silu(q_raw.astype(jnp.float32)))
    v = heads(i_raw.astype(jnp.float32))

    def gates(z, lb_d):
        z = z.astype(jnp.float32)
        f = lb_d + (1.0 - lb_d) * jax.nn.sigmoid(z)
        k = (1.0 - lb_d) * jax.nn.sigmoid(-z)
        return heads(k), heads(jnp.log(f))

    k_f, g_f = gates(f_fwd_raw, lb[0])
    k_b, g_b = gates(f_bwd_raw, lb[1])
    flip = lambda t: jnp.flip(t, axis=2)
    o = _chunk_gated_recurrence(q, k_f, v, g_f) + \
        flip(_chunk_gated_recurrence(flip(q), flip(k_b), flip(v), flip(g_b)))
    o = o * lax.rsqrt(jnp.mean(o * o, axis=-1, keepdims=True) + EPS) * norm_g.astype(jnp.float32)
    o = o.transpose(0, 2, 1, 3).reshape(B, T, HG_DIM)
    return o * jax.nn.silu(g_raw.astype(jnp.float32))


def _neighbourhood_attention(q_raw, k_raw, v_raw, rpb):
    B, T, _ = q_raw.shape
    rows = T // GRID_W
    kh = min(NA_KH, rows)
    nk = kh * NA_KB_W
    heads = lambda t: t.astype(jnp.float32).reshape(B, T, NA_HEADS, NA_HEAD_DIM).transpose(0, 2, 1, 3)
    q, k, v = heads(q_raw), heads(k_raw), heads(v_raw)

    r = np.arange(rows)
    key_rows = np.clip(r - kh // 2, 0, rows - kh)[:, None] + np.arange(kh)[None, :]
    c0 = np.arange(NA_NCB) * NA_QB_W
    key_cols = np.clip(c0 - NA_KW // 2, 0, GRID_W - NA_KB_W)[:, None] + np.arange(NA_KB_W)[None, :]
    tok = (key_rows[:, None, :, None] * GRID_W + key_cols[None, :, None, :]).reshape(rows, NA_NCB, nk)
    kg = k[:, :, tok]
    vg = v[:, :, tok]

    q_col = c0[:, None] + np.arange(NA_QB_W)[None, :]
    q_start = np.clip(q_col - NA_KW // 2, 0, GRID_W - NA_KW)
    k_col = np.tile(key_cols, (1, kh))
    valid = (k_col[:, None, :] >= q_start[:, :, None]) & (k_col[:, None, :] < q_start[:, :, None] + NA_KW)
    col_idx = np.clip(k_col[:, None, :] - q_col[:, :, None] + NA_KW - 1, 0, 2 * NA_KW - 2)
    row_idx = np.repeat(key_rows - r[:, None], NA_KB_W, axis=1) + NA_KH - 1
    bias = rpb.astype(jnp.float32)[:, row_idx[:, None, None, :], col_idx[None, :, :, :]]

    qb = q.reshape(B, NA_HEADS, rows, NA_NCB, NA_QB_W, NA_HEAD_DIM)
    s = jnp.einsum('bhrjqd,bhrjkd->bhrjqk', qb, kg) * (NA_HEAD_DIM ** -0.5) + bias[None]
    s = jnp.where(valid, s, NEG_INF)
    p = jax.nn.softmax(s, axis=-1)
    o = jnp.einsum('bhrjqk,bhrjkd->bhrjqd', p, vg)
    return o.reshape(B, NA_HEADS, T, NA_HEAD_DIM).transpose(0, 2, 1, 3).reshape(B, T, NA_DIM)


def _conformer_conv(a, gate, w_dw, b_dw, ln_g, ln_b):
    u = a.astype(jnp.float32) * jax.nn.sigmoid(gate.astype(jnp.float32))
    u = lax.conv_general_dilated(u, w_dw.astype(jnp.float32)[:, None, :], window_strides=(1,),
                                 padding=[(CONV_WIDTH // 2, CONV_WIDTH // 2)],
                                 dimension_numbers=('NWC', 'WIO', 'NWC'),
                                 feature_group_count=CONV_DIM) + b_dw.astype(jnp.float32)
    mu = jnp.mean(u, axis=-1, keepdims=True)
    var = jnp.mean(jnp.square(u - mu), axis=-1, keepdims=True)
    u = (u - mu) * lax.rsqrt(var + EPS) * ln_g.astype(jnp.float32) + ln_b.astype(jnp.float32)
    return jax.nn.silu(u)


def setup_inputs(seed: int = 0) -> dict:
    key = jax.random.key(seed)
    ks = jax.random.split(key, 16)
    nrm = lambda k, shape, scale: jax.random.normal(k, shape, jnp.float32) * scale
    return {
        "x": nrm(ks[0], (BATCH, SEQ, D_MODEL), 1.0),
        "mix_norm_g": 1.0 + nrm(ks[1], (DEPTH, D_MODEL), 0.02),
        "w_in": nrm(ks[2], (DEPTH, D_MODEL, D_IN), D_MODEL ** -0.5),
        "hg_lower_bounds": nrm(ks[3], (DEPTH, 2, HG_DIM), 0.1),
        "hg_norm_g": 1.0 + nrm(ks[4], (DEPTH, HG_HEAD_DIM), 0.02),
        "na_rpb": nrm(ks[5], (DEPTH, NA_HEADS, 2 * NA_KH - 1, 2 * NA_KW - 1), 0.02),
        "conv_w": nrm(ks[6], (DEPTH, CONV_WIDTH, CONV_DIM), CONV_WIDTH ** -0.5),
        "conv_b": nrm(ks[7], (DEPTH, CONV_DIM), 0.02),
        "conv_ln_g": 1.0 + nrm(ks[8], (DEPTH, CONV_DIM), 0.02),
        "conv_ln_b": nrm(ks[9], (DEPTH, CONV_DIM), 0.02),
        "w_out": nrm(ks[10], (DEPTH, D_MIX, D_MODEL), D_MIX ** -0.5),
        "ffn_norm_g": 1.0 + nrm(ks[11], (DEPTH, D_MODEL), 0.02),
        "w_gate_up": nrm(ks[12], (DEPTH, D_MODEL, 2 * FFN_DIM), D_MODEL ** -0.5),
        "w_down": nrm(ks[13], (DEPTH, FFN_DIM, D_MODEL), FFN_DIM ** -0.5),
        "final_norm_g": 1.0 + nrm(ks[14], (D_MODEL,), 0.02),
    }


def reference(x, mix_norm_g, w_in, hg_lower_bounds, hg_norm_g, na_rpb, conv_w, conv_b,
              conv_ln_g, conv_ln_b, w_out, ffn_norm_g, w_gate_up, w_down, final_norm_g):
    lbs = jax.nn.softmax(hg_lower_bounds.astype(jnp.float32), axis=0)
    lbs = jnp.cumsum(lbs, axis=0) - lbs[0:1]
    split_idx = []
    acc = 0
    for s in IN_SPLITS[:-1]:
        acc += s
        split_idx.append(acc)
    for l in range(DEPTH):
        h = _rmsnorm(x, mix_norm_g[l])
        p = h @ w_in[l]
        hq, hff, hfb, hi, hg, nq, nk, nv, ca, cg = jnp.split(p, split_idx, axis=-1)
        y_hg = _hgrn2(hq, hff, hfb, hi, hg, lbs[l], hg_norm_g[l])
        y_na = _neighbourhood_attention(nq, nk, nv, na_rpb[l])
        y_cv = _conformer_conv(ca, cg, conv_w[l], conv_b[l], conv_ln_g[l], conv_ln_b[l])
        mixed = jnp.concatenate([y_hg, y_na, y_cv], axis=-1).astype(x.dtype)
        x = x + mixed @ w_out[l]
        h = _rmsnorm(x, ffn_norm_g[l])
        gt, up = jnp.split(h @ w_gate_up[l], 2, axis=-1)
        x = x + (jax.nn.silu(gt) * up) @ w_down[l]
    return _rmsnorm(x, final_norm_g)
```

```python
import numpy as np
import concourse.bass as bass
import concourse.mybir as mybir
from concourse.bass_utils import run_bass_kernel_spmd

F32 = mybir.dt.float32
BF16 = mybir.dt.bfloat16
AF = mybir.ActivationFunctionType
ALU = mybir.AluOpType

T = 2048
D = 1024
KC = 8
DIN = 3840
FFN = 2816
NJ = FFN // 128
DEPTH = 2
EPS = 1e-6
NEGM = -30000.0
HCH = 64
NCH = 128 // HCH

C_MIXG = 0
C_FFNG = 16
C_HGN = 32
C_CONVW = 34
C_CONVB = 158
C_LNG = 162
C_LNB = 166
NCOLS = 170

K_IDF = 0
K_UF = 128
K_UB = 256
K_MF = 384
K_MB = 512
K_SCM = 128
NKF = 640
B_ID = 0
B_ONES = 128
B_OBD = 256
B_ZERO = 384
B_UF = 512
B_UB = 640
B_NUF = 768
B_NUB = 896
NKB = 1024


class Prog:
    ENGS = ("pe", "act", "dve", "pool", "sp")

    def __init__(self, nc):
        self.nc = nc
        self.ops = []
        self.lastw = {}
        self.readers = {}
        self.dma_cnt = {}
        self.final_waits = []
        self.last_on = {}
        self.bar = {}

    def barrier(self):
        snap = dict(self.last_on)
        for e in self.ENGS:
            self.bar[e] = set(snap.values())

    def op(self, eng, fn, reads=(), writes=(), dma_slot=None, nobar=False, cost=0.3):
        idx = len(self.ops)
        deps = set()
        if not nobar and self.bar.get(eng):
            deps |= self.bar[eng]
        for t in reads:
            w = self.lastw.get(t)
            if w is not None:
                deps.add(w)
        for t in writes:
            w = self.lastw.get(t)
            if w is not None:
                deps.add(w)
            for r in self.readers.get(t, ()):
                deps.add(r)
        rec = dict(eng=eng, fn=fn, deps=deps, dma_slot=dma_slot, dma_val=None, rawdeps=set(), cost=cost)
        for t in reads:
            w = self.lastw.get(t)
            if w is not None:
                rec["rawdeps"].add(w)
        if dma_slot is not None:
            self.dma_cnt[dma_slot] = self.dma_cnt.get(dma_slot, 0) + 1
            rec["dma_val"] = 16 * self.dma_cnt[dma_slot]
        self.ops.append(rec)
        if dma_slot is None:
            self.last_on[eng] = idx
        for t in reads:
            self.readers.setdefault(t, []).append(idx)
        for t in writes:
            self.lastw[t] = idx
            self.readers[t] = []
        return idx

    def schedule(self):
        import heapq
        ops = self.ops
        n = len(ops)
        children = [[] for _ in range(n)]
        indeg = [0] * n
        for i, o in enumerate(ops):
            o["alldeps"] = set(o["deps"])
            indeg[i] = len(o["alldeps"])
            for d in o["alldeps"]:
                children[d].append(i)
        finish = [0.0] * n
        ready = [0.0] * n
        blevel = [0.0] * n
        for i in range(n - 1, -1, -1):
            m = 0.0
            for c in children[i]:
                if blevel[c] > m:
                    m = blevel[c]
            blevel[i] = m + ops[i]["cost"] + 0.25
        prio = [(-blevel[i], i) for i in range(n)]
        tfree = {e: 0.0 for e in self.ENGS}
        dma_free = 0.0
        timeheap = {e: [] for e in self.ENGS}
        idxheap = {e: [] for e in self.ENGS}
        order = {e: [] for e in self.ENGS}
        for i, o in enumerate(ops):
            if indeg[i] == 0:
                heapq.heappush(timeheap[o["eng"]], (0.0, i))
        done = 0
        while done < n:
            best = None
            for e in self.ENGS:
                th, ih = timeheap[e], idxheap[e]
                while th and th[0][0] <= tfree[e]:
                    heapq.heappush(ih, prio[heapq.heappop(th)[1]])
                if ih:
                    cand = (tfree[e], ih[0][1], e, True)
                elif th:
                    cand = (th[0][0], th[0][1], e, False)
                else:
                    continue
                if best is None or cand[:2] < best[:2]:
                    best = cand
            start, i, e, fromidx = best
            if fromidx:
                heapq.heappop(idxheap[e])
            else:
                heapq.heappop(timeheap[e])
            o = ops[i]
            if o["dma_slot"] is not None:
                tfree[e] = start + (1.05 if e == "pool" else 0.1)
                t0 = max(start, dma_free)
                dma_free = t0 + o["cost"]
                finish[i] = dma_free + 2.0
            else:
                tfree[e] = start + o["cost"]
                finish[i] = tfree[e]
            order[e].append(i)
            done += 1
            for c in children[i]:
                hop = 0.05 if ops[c]["eng"] == e and o["dma_slot"] is None else 0.2
                r = finish[i] + hop
                if r > ready[c]:
                    ready[c] = r
                indeg[c] -= 1
                if indeg[c] == 0:
                    heapq.heappush(timeheap[ops[c]["eng"]], (ready[c], c))
        self.order = order
        self.est_time = max(finish) if n else 0.0

    def emit(self, final_ops):
        nc = self.nc
        ops = self.ops
        self.schedule()
        needed = [False] * len(ops)
        for i, o in enumerate(ops):
            keep = set()
            for d in o["deps"]:
                od = ops[d]
                if od["dma_slot"] is not None:
                    if o["dma_slot"] == od["dma_slot"] and d not in o["rawdeps"]:
                        continue
                    keep.add(d)
                    continue
                if od["eng"] == o["eng"] and o["dma_slot"] is None:
                    if o["eng"] == "pe":
                        continue
                keep.add(d)
            o["deps"] = keep
            for d in keep:
                needed[d] = True
        for d in final_ops:
            needed[d] = True
        cnt = {e: 0 for e in self.ENGS}
        dcnt = {}
        for e in self.ENGS:
            for i in self.order[e]:
                o = ops[i]
                if o["dma_slot"] is not None:
                    dcnt[o["dma_slot"]] = dcnt.get(o["dma_slot"], 0) + 1
                    o["ev"] = ("dma:" + o["dma_slot"], 16 * dcnt[o["dma_slot"]])
                else:
                    if needed[i]:
                        cnt[e] += 1
                        o["sig"] = True
                    else:
                        o["sig"] = False
                    o["ev"] = (e, cnt[e] if needed[i] else None)
        semnames = list(self.ENGS) + ["dma:" + s for s in self.dma_cnt]
        from contextlib import ExitStack
        with ExitStack() as st:
            sems = {}
            for sn in semnames:
                sems[sn] = st.enter_context(nc.semaphore("s_" + sn.replace(":", "_")))
            block = st.enter_context(nc.Block())
            per_eng = self.order

            def run(eng_name, handle):
                seen = {}
                for i in per_eng[eng_name]:
                    o = ops[i]
                    w = {}
                    for d in o["deps"]:
                        sk, v = ops[d]["ev"]
                        assert v is not None
                        if v > w.get(sk, 0):
                            w[sk] = v
                    for sk, v in w.items():
                        if seen.get(sk, 0) >= v:
                            continue
                        handle.wait_ge(sems[sk], v)
                        seen[sk] = v
                    ins = o["fn"](handle)
                    if o["dma_slot"] is not None:
                        ins.then_inc(sems["dma:" + o["dma_slot"]], 16)
                    elif o["sig"]:
                        ins.then_inc(sems[eng_name], 1)
                if eng_name == "sp":
                    w = {}
                    for d in final_ops:
                        sk, v = ops[d]["ev"]
                        if v > w.get(sk, 0):
                            w[sk] = v
                    for sk, v in w.items():
                        handle.wait_ge(sems[sk], v)

            @block.tensor
            def _(e):
                run("pe", e)

            @block.scalar
            def _(e):
                run("act", e)

            @block.vector
            def _(e):
                run("dve", e)

            @block.gpsimd
            def _(e):
                run("pool", e)

            @block.sync
            def _(e):
                run("sp", e)


def mk_ap(base, extra_off, dims):
    return bass.AP(tensor=base.tensor, offset=base.offset + extra_off,
                   ap=[list(base.ap[0])] + [list(d) for d in dims])


def build_program(stage="full", dbg=None):
    nc = bass.Bass("TRN2", target_bir_lowering=False)
    P = Prog(nc)
    x_d = nc.dram_tensor("x", [T, D], F32, kind="ExternalInput").ap()
    w_in_d = nc.dram_tensor("w_in", [DEPTH, D, DIN], F32, kind="ExternalInput").ap()
    w_out_d = nc.dram_tensor("w_out", [DEPTH, D, D], F32, kind="ExternalInput").ap()
    w_gu_d = nc.dram_tensor("w_gate_up", [DEPTH, D, 2 * FFN], F32, kind="ExternalInput").ap()
    w_dn_d = nc.dram_tensor("w_down", [DEPTH, FFN, D], F32, kind="ExternalInput").ap()
    cols_d = nc.dram_tensor("cols", [128, NCOLS], F32, kind="ExternalInput").ap()
    kf_d = nc.dram_tensor("kf", [128, NKF], F32, kind="ExternalInput").ap()
    kb_d = nc.dram_tensor("kb", [128, NKB], F32, kind="ExternalInput").ap()
    lbraw_d = nc.dram_tensor("lbraw", [128, 16], F32, kind="ExternalInput").ap()
    fng_d = nc.dram_tensor("fng", [1, D], F32, kind="ExternalInput").ap()
    nab_d = nc.dram_tensor("nab", [DEPTH, 128, 2 * 15 * 64], F32, kind="ExternalInput").ap()
    colneg_d = nc.dram_tensor("colneg", [128, 64], F32, kind="ExternalInput").ap()
    out_d = nc.dram_tensor("out", [T, D], F32, kind="ExternalOutput").ap()
    dbg_d = None
    if dbg is not None:
        dbg_d = nc.dram_tensor("dbg", [128, dbg[1]], dbg[2], kind="ExternalOutput").ap()

    from contextlib import ExitStack
    with ExitStack() as st:
        def sb(name, shape, dt):
            return st.enter_context(nc.sbuf_tensor(name, shape, dt))

        def ps(name):
            return st.enter_context(nc.psum_tensor(name, [128, 512], F32))

        xT = sb("xT", [128, KC, T], F32)
        hT = sb("hT", [128, KC, T], BF16)
        BIG = sb("BIG", [128, 41984], BF16)
        WB = [sb("WB%d" % i, [128, 6144], BF16) for i in range(2)]
        cols = sb("cols_sb", [128, NCOLS], F32)
        kf = sb("kf_sb", [128, NKF], F32)
        kb = sb("kb_sb", [128, NKB], BF16)
        lbc = sb("lbc", [128, 32], F32)
        PS = [ps("ps%d" % i) for i in range(8)]

        mixT = BIG[:, 0:KC * T].rearrange("p (k t) -> p k t", k=KC)
        SCR = BIG[:, KC * T:41984]
        SCRF = SCR.bitcast(F32)

        def colap(c):
            return cols[:, c:c + 1]

        def fsz(ap):
            n = 1
            for d in ap.shape[1:]:
                n *= int(d)
            return n

        PAGE = 256

        def pages(*aps):
            out = []
            for ap in aps:
                if ap is None or isinstance(ap, (int, float)):
                    continue
                if ap.tensor.name != "BIG":
                    continue
                esz = mybir.dt.size(ap.dtype)
                dims = [list(d) for d in ap.ap]
                pstep = int(dims[0][0])
                lo = (int(ap.offset) % pstep) * esz
                ext = 0
                for st, cnt in dims[1:]:
                    ext += (int(cnt) - 1) * abs(int(st))
                hi = lo + (ext + 1) * esz
                for k in range(lo // PAGE, (hi - 1) // PAGE + 1):
                    out.append(("pg", k))
            return out

        def ecost(eng, n):
            if eng == "act":
                return 0.2 + n * 0.00084
            if eng == "dve":
                return 0.16 + n * 0.00104
            return 0.3 + n * 0.002

        def dma(q, out, in_, slot, reads=(), writes=(), nobar=False):
            nbytes = fsz(out) * int(out.shape[0]) * 4
            return P.op(q, lambda e: e.dma_start(out=out, in_=in_), list(reads) + pages(in_), list(writes) + pages(out),
                        dma_slot=slot, nobar=nobar, cost=nbytes / 330e3)

        def mm(out, lhsT, rhs, start, stop, reads, writes, **kw):
            return P.op("pe", lambda e: e.matmul(out, lhsT, rhs, start=start, stop=stop, **kw),
                        list(reads) + pages(lhsT, rhs), writes, cost=0.03 + fsz(rhs) / 2400.0)

        def tr(out, in_, ident, reads, writes):
            return P.op("pe", lambda e: e.transpose(out, in_, ident), list(reads) + pages(in_), writes, cost=0.12)

        def act(out, in_, func, reads, writes, scale=1.0, bias=0.0, accum_out=None):
            kw = {}
            if accum_out is not None:
                kw["accum_out"] = accum_out
            return P.op("act", lambda e: e.activation(out=out, in_=in_, func=func, scale=scale, bias=bias, **kw),
                        list(reads) + pages(in_, bias if not isinstance(bias, float) else None),
                        list(writes) + pages(out, accum_out), cost=ecost("act", fsz(out)))

        def tt(eng, out, in0, in1, op, reads, writes):
            return P.op(eng, lambda e: e.tensor_tensor(out=out, in0=in0, in1=in1, op=op),
                        list(reads) + pages(in0, in1), list(writes) + pages(out), cost=ecost(eng, fsz(out)))

        def ts(eng, out, in0, s1, op0, reads, writes, s2=None, op1=None):
            if op1 is None:
                return P.op(eng, lambda e: e.tensor_scalar(out=out, in0=in0, scalar1=s1, scalar2=None, op0=op0),
                            list(reads) + pages(in0, s1), list(writes) + pages(out), cost=ecost(eng, fsz(out)))
            return P.op(eng, lambda e: e.tensor_scalar(out=out, in0=in0, scalar1=s1, scalar2=s2, op0=op0, op1=op1),
                        list(reads) + pages(in0, s1, s2), list(writes) + pages(out), cost=ecost(eng, fsz(out)))

        def stt(out, in0, scalar, in1, op0, op1, reads, writes):
            return P.op("dve", lambda e: e.scalar_tensor_tensor(out=out, in0=in0, scalar=scalar, in1=in1,
                                                                 op0=op0, op1=op1),
                        list(reads) + pages(in0, scalar, in1), list(writes) + pages(out),
                        cost=ecost("dve", fsz(out)) + 0.06)

        def cp(eng, out, in_, reads, writes):
            if eng == "act":
                return P.op("act", lambda e: e.copy(out=out, in_=in_), list(reads) + pages(in_),
                            list(writes) + pages(out), cost=ecost("act", fsz(out)))
            return P.op(eng, lambda e: e.tensor_copy(out=out, in_=in_), list(reads) + pages(in_),
                        list(writes) + pages(out), cost=ecost(eng, fsz(out)))

        wslot_ctr = [0]

        def wslot():
            i = wslot_ctr[0] % 2
            wslot_ctr[0] += 1
            return i

        dma("sp", cols[:], cols_d, "c_cols", (), ["cols"])
        dma("sp", kf[:], kf_d, "c_kf", (), ["kf"])
        kbs = SCRF[:, 6656:6656 + NKB]
        dma("sp", kbs, kb_d, "c_kb", (), ["kbs"])
        cp("dve", kb[:], kbs, ["kbs"], ["kb"])
        ident_f = kf[:, K_IDF:K_IDF + 128]
        ident_b = kb[:, B_ID:B_ID + 128]
        ones_b = kb[:, B_ONES:B_ONES + 128]

        xin = SCRF[:, 0:4096].rearrange("p (a d) -> p a d", a=4)
        for tile in range(16):
            s = tile % 4
            dma("sp", xin[:, s, :], x_d[tile * 128:(tile + 1) * 128, :], "xin%d" % s, (), [("xin", s)])
            for half in range(2):
                bank = PS[(tile * 2 + half) % 4]
                bname = "ps%d" % ((tile * 2 + half) % 4)
                for j in range(4):
                    kc = half * 4 + j
                    tr(bank[:, j * 128:(j + 1) * 128], xin[:, s, kc * 128:(kc + 1) * 128], ident_f,
                       [("xin", s), "kf"], [bname])
                eng = "dve" if half == 0 else "act"
                cp(eng, xT[:, half * 4:half * 4 + 4, tile * 128:(tile + 1) * 128],
                   bank[:].rearrange("p (j c) -> p j c", j=4), [bname],
                   [("xT", kc, tile // 4) for kc in range(half * 4, half * 4 + 4)])

        def rmsnorm(gcol0, tag):
            sq = SCR[:, 8192:12288].rearrange("p (k t) -> p k t", k=KC)
            rstd = SCRF[:, 6144:6656]
            for blk in range(4):
                tsl = slice(blk * 512, (blk + 1) * 512)
                for kc in range(KC):
                    act(sq[:, kc, :], xT[:, kc, tsl], AF.Square, [("xT", kc, blk)], [("sq", kc)])
                for kc in range(KC):
                    mm(PS[4][:], ones_b, sq[:, kc, :], kc == 0, kc == KC - 1, [("sq", kc), "kb"], ["ps4"])
                act(rstd, PS[4][:], AF.Ln, ["ps4"], ["rstd"], scale=1.0 / D, bias=EPS)
                act(rstd, rstd, AF.Exp, ["rstd"], ["rstd"], scale=-0.5)
                for kc in range(KC):
                    stt(hT[:, kc, tsl], xT[:, kc, tsl], colap(gcol0 + kc), rstd, ALU.mult, ALU.mult,
                        [("xT", kc, blk), "rstd", "cols"], [("hT", kc, blk)])

        def load_w_in(layer, groups, after=()):
            si = wslot()
            slot = WB[si]
            aps = []
            off = 0
            wv = w_in_d[layer].rearrange("(kc p) n -> p kc n", p=128)
            for (c0, n) in groups:
                ap = slot[:, off:off + KC * n].rearrange("p (k n) -> p k n", k=KC)
                dma("pool", ap, wv[:, :, c0:c0 + n], "wb%d" % si, list(after), [("WB", si)], nobar=True)
                aps.append(ap)
                off += KC * n
            return si, aps

        def proj_fm(bank, bname, w_ap, si, blk, c0=0, n=128):
            for kc in range(KC):
                mm(bank[0:n, :], w_ap[:, kc, c0:c0 + n], hT[:, kc, blk * 512:(blk + 1) * 512], kc == 0, kc == KC - 1,
                   [("WB", si), ("hT", kc, blk)], [bname])

        def lower_bounds():
            dma("sp", lbc[:, 0:16], lbraw_d, "c_lb", (), ["lbc"])
            act(lbc[:, 0:16], lbc[:, 0:16], AF.Exp, ["lbc"], ["lbc"])
            tt("dve", lbc[:, 24:32], lbc[:, 0:8], lbc[:, 8:16], ALU.add, ["lbc"], ["lbc"])
            P.op("dve", lambda e: e.reciprocal(out=lbc[:, 24:32], in_=lbc[:, 24:32]), ["lbc"], ["lbc"])
            tt("dve", lbc[:, 16:24], lbc[:, 8:16], lbc[:, 24:32], ALU.mult, ["lbc"], ["lbc"])
            ts("dve", lbc[:, 24:32], lbc[:, 16:24], -1.0, ALU.mult, ["lbc"], ["lbc"], s2=1.0, op1=ALU.add)

        def conv_mixer(layer):
            CH = 2
            ub = SCR[:, 0:CH * 2080].rearrange("p (c t) -> p c t", c=CH)
            dg = SCR[:, 4160:4160 + 62 * 128].rearrange("p (c j k) -> p c j k", c=CH, j=31)
            FB = 6080
            a2 = SCRF[:, FB:FB + 1024].rearrange("p (c t) -> p c t", c=CH)
            tmp = SCRF[:, FB + 1024:FB + 1536]
            tmp2 = SCRF[:, FB + 1536:FB + 2048]
            sg = SCRF[:, FB + 2048:FB + 2560]
            sqb = SCR[:, 2 * (FB + 2560):2 * (FB + 2560) + 1024].rearrange("p (c t) -> p c t", c=CH)
            accb = SCR[:, 2 * (FB + 2560) + 1024:2 * (FB + 2560) + 2048].rearrange("p (c t) -> p c t", c=CH)
            si, (wa, wg) = load_w_in(layer, [(3328, 256), (3584, 256)],
                                     after=[("xin", i) for i in range(4)] if layer == 0 else ())
            U = lambda c: [("u", c, b) for b in (-1, 0, 1, 2, 3, 4)]
            for c in range(CH):
                P.op("pool", lambda e, c=c: e.memset(ub[:, c, 0:16], 0.0), (), [("u", c, -1)] + pages(ub[:, c, 0:16]))
                P.op("pool", lambda e, c=c: e.memset(ub[:, c, 16 + T:2080], 0.0), (),
                     [("u", c, 4)] + pages(ub[:, c, 16 + T:2080]))
                wc0 = C_CONVW + (layer * 2 + c) * 31
                tt("dve", dg[:, c, :, :], mk_ap(ident_b, 0, [[0, 31], [1, 128]]),
                   mk_ap(cols[:, wc0:wc0 + 31], 0, [[1, 31], [0, 128]]), ALU.mult, ["kb", "cols"], [("dg", c)])
            for c in range(CH):
                for blk in range(4):
                    proj_fm(PS[0], "ps0", wa, si, blk, c * 128)
                    proj_fm(PS[1], "ps1", wg, si, blk, c * 128)
                    act(tmp, PS[1][:], AF.Exp, ["ps1"], ["ctmp"], scale=-1.0)
                    act(tmp, tmp, AF.Ln, ["ctmp"], ["ctmp"], bias=1.0)
                    act(tmp, tmp, AF.Exp, ["ctmp"], ["ctmp"], scale=-1.0)
                    tt("dve", ub[:, c, 16 + blk * 512:16 + (blk + 1) * 512], PS[0][:], tmp, ALU.mult,
                       ["ps0", "ctmp"], [("u", c, blk)])
            for blk in range(4):
                tsl = slice(blk * 512, (blk + 1) * 512)
                for c in range(CH):
                    bank, bname = PS[2 + c], "ps%d" % (2 + c)
                    for j in range(31):
                        mm(bank[:], dg[:, c, j, :], ub[:, c, blk * 512 + j + 1:blk * 512 + j + 1 + 512], j == 0, j == 30,
                           [("dg", c)] + [("u", c, b) for b in (blk - 1, blk, blk + 1)], [bname])
                    act(a2[:, c, :], bank[:], AF.Identity, [bname, "cols"], [("a2", c)],
                        bias=colap(C_CONVB + layer * 2 + c))
                for c in range(CH):
                    act(sqb[:, c, :], a2[:, c, :], AF.Square, [("a2", c)], [("csq", c)])
                    cp("dve", accb[:, c, :], a2[:, c, :], [("a2", c)], [("cab", c)])
                for c in range(CH):
                    mm(PS[4][:], ones_b, accb[:, c, :], c == 0, c == CH - 1, [("cab", c), "kb"], ["ps4"])
                for c in range(CH):
                    mm(PS[5][:], ones_b, sqb[:, c, :], c == 0, c == CH - 1, [("csq", c), "kb"], ["ps5"])
                ts("dve", tmp, PS[4][:], 1.0 / 256, ALU.mult, ["ps4"], ["ctmp"])
                tt("dve", tmp2, tmp, tmp, ALU.mult, ["ctmp"], ["ctmp2"])
                stt(tmp2, PS[5][:], 1.0 / 256, tmp2, ALU.mult, ALU.subtract, ["ps5", "ctmp2"], ["ctmp2"])
                act(tmp2, tmp2, AF.Ln, ["ctmp2"], ["ctmp2"], bias=EPS)
                act(tmp2, tmp2, AF.Exp, ["ctmp2"], ["ctmp2"], scale=-0.5)
                for c in range(CH):
                    a = a2[:, c, :]
                    A = ("a2", c)
                    tt("dve", a, a, tmp, ALU.subtract, [A, "ctmp"], [A])
                    tt("dve", a, a, tmp2, ALU.mult, [A, "ctmp2"], [A])
                    ts("dve", a, a, colap(C_LNG + layer * 2 + c), ALU.mult, [A, "cols"], [A],
                       s2=colap(C_LNB + layer * 2 + c), op1=ALU.add)
                    act(sg, a, AF.Exp, [A], ["csg"], scale=-1.0)
                    act(sg, sg, AF.Ln, ["csg"], ["csg"], bias=1.0)
                    act(sg, sg, AF.Exp, ["csg"], ["csg"], scale=-1.0)
                    tt("dve", mixT[:, 6 + c, tsl], a, sg, ALU.mult, [A, "csg"], [("mixT", 6 + c, blk)])

        def kr0(r):
            return min(max(r - 4, 0), 24)

        def na_mixer(layer):
            si, (wq, wk, wv) = load_w_in(layer, [(2560, 256), (2816, 256), (3072, 256)],
                                         after=[("xin", i) for i in range(4)] if layer == 0 else ())
            qT = SCR[:, 0:2048]
            Kbd = SCR[:, 2048:6144].rearrange("p (r k) -> p r k", r=32)
            Vbd = SCR[:, 6144:10240].rearrange("p (r k) -> p r k", r=32)
            tblf = SCRF[:, 5120:7040].rearrange("p (a t q) -> p a t q", a=2, t=15)
            tblf3 = SCRF[:, 5120:7040].rearrange("p (a q) -> p a q", q=64)
            PT = [SCR[:, 14080 + i * 512:14080 + (i + 1) * 512] for i in range(2)]
            rec = SCRF[:, 7552:8064]
            TMP = [SCRF[:, 8064 + i * 512:8064 + (i + 1) * 512] for i in range(2)]
            cng = SCRF[:, 9088:9152]
            VTbd_flat = SCR[:, 18304:22400]
            VTbd = VTbd_flat.rearrange("p (r k) -> p r k", r=32)
            dma("sp", SCRF[:, 5120:7040], nab_d[layer], "c_nab", (), ["tblf"])
            dma("sp", cng, colneg_d, "c_cng", (), ["cng"])
            tt("dve", tblf3, tblf3, mk_ap(cng, 0, [[0, 30], [1, 64]]), ALU.add, ["tblf", "cng"], ["tblf"])
            obd = kb[:, B_OBD:B_OBD + 128]
            zer = kb[:, B_ZERO:B_ZERO + 128]
            P.op("pool", lambda e: e.memset(SCR[:, 2048:6144], 0.0), (), ["Kbd"] + pages(SCR[:, 2048:6144]), cost=3.5)
            P.op("pool", lambda e: e.memset(VTbd_flat, 0.0), (), ["VTbd"] + pages(VTbd_flat), cost=3.5)
            for pr in range(2):
                for blk in range(4):
                    tsl = slice(blk * 512, (blk + 1) * 512)
                    proj_fm(PS[0], "ps0", wq, si, blk, pr * 128)
                    cp("act", qT[:, tsl], PS[0][:], ["ps0"], [("qT", blk)])
                    proj_fm(PS[1], "ps1", wk, si, blk, pr * 128)
                    for a_ in range(2):
                        cp("dve" if a_ == 0 else "act",
                           Kbd[a_ * 64:(a_ + 1) * 64, blk * 8:(blk + 1) * 8, a_ * 64:(a_ + 1) * 64],
                           PS[1][a_ * 64:(a_ + 1) * 64, :].rearrange("p (r k) -> p r k", r=8), ["ps1"], ["Kbd"])
                for blk in range(4):
                    proj_fm(PS[2], "ps2", wv, si, blk, pr * 128)
                    for a_ in range(2):
                        cp("dve" if a_ == 0 else "act",
                           VTbd[a_ * 64:(a_ + 1) * 64, blk * 8:(blk + 1) * 8, a_ * 64:(a_ + 1) * 64],
                           PS[2][a_ * 64:(a_ + 1) * 64, :].rearrange("p (r k) -> p r k", r=8), ["ps2"], ["VTbd"])
                for kr8 in range(4):
                    bank, bname = PS[3], "ps3"
                    bankb = bank[:].bitcast(BF16)
                    for i8 in range(8):
                        kr = kr8 * 8 + i8
                        tr(bankb[:, i8 * 128:(i8 + 1) * 128], VTbd[:, kr, :], ident_b, ["VTbd", "kb"], [bname])
                    cp("dve" if kr8 % 2 == 0 else "act", Vbd[:, kr8 * 8:(kr8 + 1) * 8, :],
                       bankb.rearrange("p (r k) -> p r k", r=8), [bname], ["Vbd"])
                unit = 0
                for g in range(4):
                    r0 = 8 * g
                    Ob, Obn = PS[4 + (g % 2) * 2], "ps%d" % (4 + (g % 2) * 2)
                    Db, Dbn = PS[5 + (g % 2) * 2], "ps%d" % (5 + (g % 2) * 2)
                    mm(Ob[:], zer, qT[:, 0:512], True, False, ["kb", ("qT", 0)], [Obn])
                    mm(Db[:], zer, qT[:, 0:512], True, False, ["kb", ("qT", 0)], [Dbn])
                    krs = list(range(kr0(r0), kr0(r0 + 7) + 8))
                    for ki, kr in enumerate(krs):
                        rows = [r for r in range(r0, r0 + 8) if kr0(r) <= kr <= kr0(r) + 7]
                        ra, rb = rows[0], rows[-1]
                        n = (rb - ra + 1) * 64
                        t0 = ra - kr + 7
                        c0 = (ra - r0) * 64
                        sbi = unit % 2
                        unit += 1
                        Sb_, Sbn = PS[sbi], "ps%d" % sbi
                        pt, tmp = PT[sbi], TMP[sbi]
                        qdeps = sorted(set([("qT", (ra * 64) // 512), ("qT", (rb * 64 + 63) // 512)]))
                        mm(Sb_[:, 0:n], Kbd[:, kr, :], qT[:, ra * 64:ra * 64 + n], True, True, ["Kbd"] + qdeps, [Sbn])
                        stt(tmp[:, 0:n], Sb_[:, 0:n], 0.125,
                            tblf[:, pr, t0:t0 + (rb - ra + 1), :].rearrange("p t q -> p (t q)"),
                            ALU.mult, ALU.add, [Sbn, "tblf"], [("natmp", sbi)])
                        act(pt[:, 0:n], tmp[:, 0:n], AF.Exp, [("natmp", sbi)], [("PT", sbi)])
                        last = ki == len(krs) - 1
                        mm(Ob[:, c0:c0 + n], Vbd[:, kr, :], pt[:, 0:n], False, last, ["Vbd", ("PT", sbi)], [Obn])
                        mm(Db[:, c0:c0 + n], obd, pt[:, 0:n], False, last, ["kb", ("PT", sbi)], [Dbn])
                    act(rec, Db[:], AF.Ln, [Dbn], ["narec"])
                    act(rec, rec, AF.Exp, ["narec"], ["narec"], scale=-1.0)
                    tt("dve", mixT[:, 4 + pr, r0 * 64:r0 * 64 + 512], Ob[:], rec, ALU.mult, [Obn, "narec"],
                       [("mixT", 4 + pr, g)])

        def hgrn2_head(layer, h):
            si, (wq, wff, wfb, wi, wg) = load_w_in(layer, [(h * 128, 128), (512 + h * 128, 128), (1024 + h * 128, 128),
                                                           (1536 + h * 128, 128), (2048 + h * 128, 128)])
            W = ("WB", si)
            sqT = SCR[:, 0:2048]
            vtm = SCR[:, 2048:4096].rearrange("p (t d) -> p t d", t=16)
            oacc = SCRF[:, 2048:4096]
            AE = []
            for i in range(2):
                b0 = 4096 + i * 2560
                AE.append([SCRF[:, b0 + j * 512:b0 + (j + 1) * 512] for j in range(5)])
            kin = SCR[:, 2 * 9216:2 * 9216 + 512]
            kkT = SCR[:, 2 * 9472:2 * 9472 + 512]
            HO = []
            for i in range(2):
                fb = 9728 + i * 1088
                HO.append(dict(qin=SCR[:, 2 * fb:2 * fb + 512], PTm=SCR[:, 2 * fb + 512:2 * fb + 1024],
                               kkx=SCR[:, 2 * fb + 1024:2 * fb + 1024 + 512 * NCH].rearrange("p (t c d) -> p t c d", t=4, c=NCH),
                               kkx_flat=SCR[:, 2 * fb + 1024:2 * fb + 1024 + 512 * NCH],
                               dec=SCRF[:, fb + 1024:fb + 1040], i=i))
            X = [SCRF[:, 11904 + i * 256:11904 + (i + 1) * 256] for i in range(2)]
            Sbb = [SCR[:, 2 * 12416 + i * 256:2 * 12416 + (i + 1) * 256] for i in range(2)]
            zer = kb[:, B_ZERO:B_ZERO + 128]
            scm = kf[:, K_SCM:K_SCM + 512]
            hgn = colap(C_HGN + layer)
            for blk in range(4):
                tsl = slice(blk * 512, (blk + 1) * 512)
                proj_fm(PS[6], "ps6", wq, si, blk)
                A_ = AE[blk % 2][0]
                An = "A%d" % (blk % 2)
                act(A_, PS[6][:], AF.Exp, ["ps6"], [An], scale=-1.0)
                act(A_, A_, AF.Ln, [An], [An], bias=1.0)
                act(A_, A_, AF.Exp, [An], [An], scale=-1.0)
                tt("dve", sqT[:, tsl], PS[6][:], A_, ALU.mult, ["ps6", An], [("sqT", blk)])
                for t4 in range(4):
                    tile = blk * 4 + t4
                    for kc in range(KC):
                        mm(PS[7][:, t4 * 128:(t4 + 1) * 128], hT[:, kc, tile * 128:(tile + 1) * 128], wi[:, kc, :],
                           kc == 0, kc == KC - 1, [W, ("hT", kc, blk)], ["ps7"])
                cp("dve", vtm[:, blk * 4:(blk + 1) * 4, :], PS[7][:].rearrange("p (t d) -> p t d", t=4), ["ps7"],
                   [("vtm", blk)])
            for ho in HO:
                P.op("pool", lambda e, ho=ho: e.memset(ho["kkx_flat"], 0.0), (), [("kkx", ho["i"])] + pages(ho["kkx_flat"]),
                     cost=1.0)

            touched = set()
            state = dict(xi=0, prevSb=None)

            def front(u):
                dirn, blk, ho = u["dirn"], u["blk"], HO[u["hi"]]
                hi = u["hi"]
                tsl = slice(blk * 512, (blk + 1) * 512)
                wf_ = wfb if dirn else wff
                A_, B_, C_, D_, E_ = AE[u["ae"]]
                nA, nB, nC, nD, nE = ["%s%d" % (x, u["ae"]) for x in "ABCDE"]
                proj_fm(PS[0], "ps0", wf_, si, blk)
                act(B_, PS[0][:], AF.Exp, ["ps0"], [nB], scale=-1.0)
                act(B_, B_, AF.Ln, [nB], [nB], bias=1.0)
                act(A_, B_, AF.Exp, [nB], [nA], scale=-1.0)
                if layer == 0:
                    sg = -1.0
                else:
                    ci = dirn * 4 + h
                    ts("dve", A_, A_, lbc[:, 24 + ci:25 + ci], ALU.mult, [nA, "lbc"], [nA],
                       s2=lbc[:, 16 + ci:17 + ci], op1=ALU.add)
                    act(B_, A_, AF.Ln, [nA], [nB])
                    sg = 1.0
                P.op("dve", lambda e: e.tensor_tensor_scan(out=C_, data0=scm, data1=B_, initial=0.0,
                                                          op0=ALU.mult, op1=ALU.add), [nB, "kf"] + pages(B_), [nC] + pages(C_), cost=1.25)
                blast = mk_ap(C_, HCH - 1, [[HCH, 512 // HCH], [0, HCH]])
                c3 = C_.rearrange("p (n c) -> p n c", c=HCH)
                d3 = D_.rearrange("p (n c) -> p n c", c=HCH)
                act(ho["dec"][:, 0:512 // HCH], mk_ap(C_, HCH - 1, [[HCH, 512 // HCH]]), AF.Exp, [nC], [("dec", hi)],
                    scale=sg)
                if dirn == 0:
                    act(E_, C_, AF.Exp, [nC], [nE], scale=-sg)
                    act(C_, C_, AF.Exp, [nC], [nC], scale=sg)
                    Eb, Enb = C_, E_
                    ebn, enbn = nC, nE
                else:
                    tt("dve", d3, blast, c3, ALU.subtract, [nC], [nD])
                    tt("dve", E_, B_, D_, ALU.add, [nB, nD], [nE])
                    act(C_, E_, AF.Exp, [nE], [nC], scale=-sg)
                    act(E_, E_, AF.Exp, [nE], [nE], scale=sg)
                    Eb, Enb = E_, C_
                    ebn, enbn = nE, nC
                tt("dve", ho["qin"], sqT[:, tsl], Eb, ALU.mult, [("sqT", blk), ebn], [("qin", hi)])
                stt(D_, A_, 1.0, Enb, ALU.subtract, ALU.mult, [nA, enbn], [nD])
                cp("dve", kin, D_, [nD], ["kin"])
                tt("dve", kkT.rearrange("p (n c) -> p n c", c=HCH), d3,
                   mk_ap(ho["dec"], 0, [[1, 512 // HCH], [0, HCH]]), ALU.mult, [nD, ("dec", hi)], ["kkT"])
                kb_ = PS[1][:].bitcast(BF16)
                for t4 in range(4):
                    cs = slice(t4 * 128, (t4 + 1) * 128)
                    tr(kb_[:, cs], kkT[:, cs], ident_b, ["kkT", "kb"], ["ps1"])
                for ch in range(NCH):
                    psl = slice(ch * HCH, (ch + 1) * HCH)
                    cp("act", ho["kkx"][psl, :, ch, :],
                       kb_[psl, 0:512].rearrange("p (t d) -> p t d", t=4), ["ps1"], [("kkx", hi)])
                for t4 in range(4):
                    cs = slice(t4 * 128, (t4 + 1) * 128)
                    mm(PS[2][:, cs], kin[:, cs], ho["qin"][:, cs], True, True, ["kin", ("qin", hi)], ["ps2"])
                Umask = kb[:, (B_NUB if dirn else B_NUF):(B_NUB if dirn else B_NUF) + 128]
                tt("dve", ho["PTm"].rearrange("p (t c) -> p t c", t=4), PS[2][:].rearrange("p (t c) -> p t c", t=4),
                   mk_ap(Umask, 0, [[0, 4], [1, 128]]), ALU.mult, ["ps2", "kb"], [("PTm", hi)])

            def back(u):
                dirn, blk, ho = u["dirn"], u["blk"], HO[u["hi"]]
                hi = u["hi"]
                A_, B_, C_, D_, E_ = AE[u["ae"]]
                nA, nB, nC, nD, nE = ["%s%d" % (x, u["ae"]) for x in "ABCDE"]
                tsl = slice(blk * 512, (blk + 1) * 512)
                order = [3, 2, 1, 0] if dirn else [0, 1, 2, 3]
                corder = list(range(NCH - 1, -1, -1)) if dirn else list(range(NCH))
                if u["first"]:
                    state["prevSb"] = None
                for t4 in order:
                    tile = blk * 4 + t4
                    cs = slice(t4 * 128, (t4 + 1) * 128)
                    dsb, dsn = PS[3 + state["xi"] % 2], "ps%d" % (3 + state["xi"] % 2)
                    xi = state["xi"] % 2
                    Xc, Xp = X[xi], X[1 - xi]
                    for ch in range(NCH):
                        mm(dsb[:, ch * 128:(ch + 1) * 128], ho["kkx"][:, t4, ch, :], vtm[:, tile, :], True, True,
                           [("kkx", hi), ("vtm", blk)], [dsn])
                    for n_, ch in enumerate(corder):
                        c0 = t4 * NCH + ch
                        dcol = ho["dec"][:, c0:c0 + 1]
                        xo = Xc[:, ch * 128:(ch + 1) * 128]
                        if n_ == 0:
                            if state["prevSb"] is None:
                                ts("dve", xo, dsb[:, ch * 128:(ch + 1) * 128], -1.0, ALU.mult, [dsn], [("X", xi)])
                            else:
                                lastch = corder[-1]
                                stt(xo, Xp[:, lastch * 128:(lastch + 1) * 128], dcol, dsb[:, ch * 128:(ch + 1) * 128],
                                    ALU.mult, ALU.subtract, [("X", 1 - xi), ("dec", hi), dsn], [("X", xi)])
                        else:
                            pch = corder[n_ - 1]
                            stt(xo, Xc[:, pch * 128:(pch + 1) * 128], dcol, dsb[:, ch * 128:(ch + 1) * 128],
                                ALU.mult, ALU.subtract, [("X", xi), ("dec", hi), dsn], [("X", xi)])
                    cp("act", Sbb[xi][:, 0:NCH * 128], Xc[:, 0:NCH * 128], [("X", xi)], [("Sb", xi)])
                    mm(PS[5][:, cs], vtm[:, tile, :], ho["PTm"][:, cs], True, False, [("vtm", blk), ("PTm", hi)], ["ps5"])
                    for n_, ch in enumerate(corder):
                        c0 = t4 * 128 + ch * HCH
                        if n_ == 0:
                            if state["prevSb"] is None:
                                lhs, ldep = zer, "kb"
                            else:
                                lastch = corder[-1]
                                lhs, ldep = Sbb[1 - xi][:, lastch * 128:(lastch + 1) * 128], ("Sb", 1 - xi)
                        else:
                            pch = corder[n_ - 1]
                            lhs, ldep = Sbb[xi][:, pch * 128:(pch + 1) * 128], ("Sb", xi)
                        mm(PS[5][:, c0:c0 + HCH], lhs, ho["qin"][:, c0:c0 + HCH], False, n_ == NCH - 1, [ldep, ("qin", hi)], ["ps5"])
                    state["prevSb"] = xi
                    state["xi"] += 1
                if blk not in touched:
                    touched.add(blk)
                    cp("act", oacc[:, tsl], PS[5][:], ["ps5"], [("oacc", blk)])
                else:
                    tt("dve", oacc[:, tsl], oacc[:, tsl], PS[5][:], ALU.add, ["ps5", ("oacc", blk)], [("oacc", blk)])
                    osq = C_.bitcast(BF16)[:, 0:512]
                    act(osq, oacc[:, tsl], AF.Square, [("oacc", blk)], [nC])
                    mm(PS[6][:], ones_b, osq, True, True, [nC, "kb"], ["ps6"])
                    act(B_, PS[6][:], AF.Ln, ["ps6"], [nB], scale=1.0 / 128, bias=EPS)
                    act(B_, B_, AF.Exp, [nB], [nB], scale=-0.5)
                    proj_fm(PS[7], "ps7", wg, si, blk)
                    act(A_, PS[7][:], AF.Exp, ["ps7"], [nA], scale=-1.0)
                    act(A_, A_, AF.Ln, [nA], [nA], bias=1.0)
                    act(A_, A_, AF.Exp, [nA], [nA], scale=-1.0)
                    tt("dve", A_, A_, PS[7][:], ALU.mult, [nA, "ps7"], [nA])
                    tt("dve", oacc[:, tsl], oacc[:, tsl], B_, ALU.mult, [("oacc", blk), nB], [("oacc", blk)])
                    stt(mixT[:, h, tsl], oacc[:, tsl], hgn, A_, ALU.mult, ALU.mult, [("oacc", blk), "cols", nA],
                        [("mixT", h, blk)])

            units = []
            for dirn in (1, 0):
                for n_, blk in enumerate([3, 2, 1, 0] if dirn else [0, 1, 2, 3]):
                    units.append(dict(dirn=dirn, blk=blk, first=(n_ == 0), hi=len(units) % 2, ae=len(units) % 2))
            front(units[0])
            for i, u in enumerate(units):
                if i + 1 < len(units):
                    front(units[i + 1])
                back(u)

        def out_proj(layer):
            wv = w_out_d[layer].rearrange("(kc p) n -> p kc n", p=128)
            waps = []
            for halfn in range(2):
                si = wslot()
                wap = WB[si][:, 0:KC * 512].rearrange("p (k n) -> p k n", k=KC)
                dma("pool", wap, wv[:, :, halfn * 512:(halfn + 1) * 512], "wb%d" % si, (), [("WB", si)], nobar=True)
                waps.append((si, wap))
            cnt = 0
            for blk in range(4):
                tsl = slice(blk * 512, (blk + 1) * 512)
                for fc in range(KC):
                    si, wap = waps[fc // 4]
                    fc4 = fc % 4
                    bi = cnt % 4
                    cnt += 1
                    bank, bname = PS[bi], "ps%d" % bi
                    for kc in range(KC):
                        mm(bank[:], wap[:, kc, fc4 * 128:(fc4 + 1) * 128], mixT[:, kc, tsl], kc == 0, kc == KC - 1,
                           [("WB", si), ("mixT", kc, blk)], [bname])
                    tt("dve", xT[:, fc, tsl], xT[:, fc, tsl], bank[:], ALU.add, [bname, ("xT", fc, blk)],
                       [("xT", fc, blk)])

        def ffn(layer):
            actT = BIG[:, 0:NJ * 1024].rearrange("p (j t) -> p j t", j=NJ)
            gv = w_gu_d[layer].rearrange("(kc p) n -> p kc n", p=128)
            dv = w_dn_d[layer].rearrange("(j p) n -> p j n", p=128)
            SIL = [SCRF[:, 9216 + i * 512:9216 + (i + 1) * 512] for i in range(2)]
            AT = [("BIGall",)]
            cnt = 0
            for half in range(2):
                for sl in range(NJ // 2):
                    si = wslot()
                    gap = WB[si][:, 0:2048].rearrange("p (k n) -> p k n", k=KC)
                    uap = WB[si][:, 2048:4096].rearrange("p (k n) -> p k n", k=KC)
                    dma("pool", gap, gv[:, :, sl * 256:(sl + 1) * 256], "wb%d" % si, (), [("WB", si)], nobar=True)
                    dma("pool", uap, gv[:, :, FFN + sl * 256:FFN + (sl + 1) * 256], "wb%d" % si, (), [("WB", si)], nobar=True)
                    for jj in range(2):
                        j = sl * 2 + jj
                        for b2 in range(2):
                            blk = half * 2 + b2
                            tsl = slice(blk * 512, (blk + 1) * 512)
                            bi = cnt % 2
                            cnt += 1
                            bg, bgn = PS[bi * 2], "ps%d" % (bi * 2)
                            bu, bun = PS[bi * 2 + 1], "ps%d" % (bi * 2 + 1)
                            for kc in range(KC):
                                mm(bg[:], gap[:, kc, jj * 128:(jj + 1) * 128], hT[:, kc, tsl], kc == 0, kc == KC - 1,
                                   [("WB", si), ("hT", kc, blk)], [bgn])
                            for kc in range(KC):
                                mm(bu[:], uap[:, kc, jj * 128:(jj + 1) * 128], hT[:, kc, tsl], kc == 0, kc == KC - 1,
                                   [("WB", si), ("hT", kc, blk)], [bun])
                            act(SIL[bi], bg[:], AF.Silu, [bgn], [("sil", bi)])
                            tt("dve", actT[:, j, b2 * 512:(b2 + 1) * 512], SIL[bi], bu[:], ALU.mult, [("sil", bi), bun],
                               [("actT", j, b2)] + mixall)
                for fp in range(4):
                    si = wslot()
                    dap = WB[si][:, 0:NJ * 256].rearrange("p (j n) -> p j n", j=NJ)
                    dma("pool", dap, dv[:, :, fp * 256:(fp + 1) * 256], "wb%d" % si, (), [("WB", si)], nobar=True)
                    for f2 in range(2):
                        fc = fp * 2 + f2
                        for b2 in range(2):
                            blk = half * 2 + b2
                            tsl = slice(blk * 512, (blk + 1) * 512)
                            bi = 4 + (cnt % 2)
                            cnt += 1
                            bank, bname = PS[bi], "ps%d" % bi
                            for j in range(NJ):
                                mm(bank[:], dap[:, j, f2 * 128:(f2 + 1) * 128], actT[:, j, b2 * 512:(b2 + 1) * 512],
                                   j == 0, j == NJ - 1, [("WB", si), ("actT", j, b2)], [bname])
                            tt("dve", xT[:, fc, tsl], xT[:, fc, tsl], bank[:], ALU.add, [bname, ("xT", fc, blk)],
                               [("xT", fc, blk)])

        mixall = [("mixT", kc, blk) for kc in range(KC) for blk in range(4)]

        def final_out():
            gbc = SCRF[:, 3072:4096]
            junk = [SCRF[:, 4096 + i * 512:4096 + (i + 1) * 512] for i in range(2)]
            ot = [SCRF[:, 5120 + i * 1024:5120 + (i + 1) * 1024] for i in range(4)]
            ssall = SCRF[:, 12288:12304]
            dma("sp", gbc, fng_d[0].partition_broadcast(128), "c3", (), ["gbc"])
            fin = []
            for tile in range(16):
                o = tile % 4
                ss = ssall[:, o * 4:o * 4 + 4]
                bp = 3 if tile < 8 else o
                banks = [(PS[bp * 2 + hf], "ps%d" % (bp * 2 + hf)) for hf in range(2)]
                for hf in range(2):
                    bank, bname = banks[hf]
                    for j in range(4):
                        kc = hf * 4 + j
                        tr(bank[:, j * 128:(j + 1) * 128], xT[:, kc, tile * 128:(tile + 1) * 128], ident_f,
                           [("xT", kc, tile // 4), "kf"], [bname])
                    act(junk[hf], bank[:], AF.Square, [bname], [("junk", hf), ("ss", o, hf)], accum_out=ss[:, hf:hf + 1])
                tt("dve", ss[:, 2:3], ss[:, 0:1], ss[:, 1:2], ALU.add, [("ss", o, 0), ("ss", o, 1)], [("ss", o, 2)])
                act(ss[:, 2:3], ss[:, 2:3], AF.Ln, [("ss", o, 2)], [("ss", o, 2)], scale=1.0 / D, bias=EPS)
                act(ss[:, 3:4], ss[:, 2:3], AF.Exp, [("ss", o, 2)], [("ss", o, 3)], scale=-0.5)
                for hf in range(2):
                    bank, bname = banks[hf]
                    stt(ot[o][:, hf * 512:(hf + 1) * 512], bank[:], ss[:, 3:4], gbc[:, hf * 512:(hf + 1) * 512],
                        ALU.mult, ALU.mult, [bname, ("ss", o, 3), "gbc"], [("ot", o)])
                fin.append(dma("sp", out_d[tile * 128:(tile + 1) * 128, :], ot[o], "out%d" % o, [("ot", o)], []))
            return fin

        final_ops = []
        if stage in ("full", "mix", "l0", "hg"):
            lower_bounds()
        nlayers = DEPTH if stage == "full" else 1
        if stage in ("conv", "na", "hg"):
            P.op("pool", lambda e: e.memset(BIG[:, 0:KC * T], 0.0), (), mixall + pages(BIG[:, 0:KC * T]))
        for layer in range(nlayers):
            rmsnorm(C_MIXG + layer * KC, "m%d" % layer)
            if stage == "norm":
                break
            if stage in ("full", "mix", "l0", "conv"):
                conv_mixer(layer)
            if stage in ("full", "mix", "l0", "na"):
                na_mixer(layer)
            if stage in ("full", "mix", "l0", "hg"):
                for h in range(4):
                    hgrn2_head(layer, h)
            if stage in ("mix", "conv", "na", "hg"):
                break
            out_proj(layer)
            rmsnorm(C_FFNG + layer * KC, "f%d" % layer)
            ffn(layer)
        if dbg is not None:
            name = dbg[0]
            if name == "hT":
                src = hT[:].rearrange("p k t -> p (k t)")
                rd = [("hT", kc, b) for kc in range(KC) for b in range(4)]
            elif name == "mixT":
                src = mixT.rearrange("p k t -> p (k t)")
                rd = mixall
            elif name == "xT":
                src = xT[:].rearrange("p k t -> p (k t)")
                rd = [("xT", kc, b) for kc in range(KC) for b in range(4)]
            final_ops.append(dma("sp", dbg_d, src, "dbg", rd, []))
        if stage in ("full", "l0"):
            final_ops += final_out()
        P.emit(final_ops)
    return nc


def _consts():
    kfc = np.zeros((128, NKF), np.float32)
    s = np.arange(128)[:, None]
    c = np.arange(128)[None, :]
    same = (s // HCH) == (c // HCH)
    kfc[:, K_IDF:K_IDF + 128] = np.eye(128, dtype=np.float32)
    _unused_K_UF = (same & (s <= c))
    _unused_K_UB = (same & (s >= c))
    _unused_K_MF = (same & (s > c))
    _unused_K_MB = (same & (s < c))
    kfc[:, K_SCM:K_SCM + 512] = ((np.arange(512) % HCH) != 0)[None, :]
    kbc = np.zeros((128, NKB), np.float32)
    kbc[:, B_ID:B_ID + 128] = np.eye(128, dtype=np.float32)
    kbc[:, B_ONES:B_ONES + 128] = 1.0
    kbc[:, B_OBD:B_OBD + 128] = ((s // 64) == (c // 64))
    kbc[:, B_UF:B_UF + 128] = (same & (s <= c))
    kbc[:, B_UB:B_UB + 128] = (same & (s >= c))
    kbc[:, B_NUF:B_NUF + 128] = -1.0 * (same & (s <= c))
    kbc[:, B_NUB:B_NUB + 128] = -1.0 * (same & (s >= c))
    kcol = np.arange(64)[:, None]
    qcol = np.arange(64)[None, :]
    qs = np.clip(qcol - 8, 0, 48)
    valid = (kcol >= qs) & (kcol < qs + 16)
    colneg = np.where(valid, 0.0, NEGM).astype(np.float32)
    colneg = np.concatenate([colneg, colneg], axis=0)
    return kfc, kbc, colneg


def _layout_small(inp):
    cols = np.zeros((128, NCOLS), np.float32)
    for l in range(DEPTH):
        cols[:, C_MIXG + l * 8:C_MIXG + (l + 1) * 8] = inp["mix_norm_g"][l].reshape(8, 128).T
        cols[:, C_FFNG + l * 8:C_FFNG + (l + 1) * 8] = inp["ffn_norm_g"][l].reshape(8, 128).T
        cols[:, C_HGN + l] = inp["hg_norm_g"][l]
        for c in range(2):
            cols[:, C_CONVW + (l * 2 + c) * 31:C_CONVW + (l * 2 + c + 1) * 31] = inp["conv_w"][l][:, c * 128:(c + 1) * 128].T
            cols[:, C_CONVB + l * 2 + c] = inp["conv_b"][l][c * 128:(c + 1) * 128]
            cols[:, C_LNG + l * 2 + c] = inp["conv_ln_g"][l][c * 128:(c + 1) * 128]
            cols[:, C_LNB + l * 2 + c] = inp["conv_ln_b"][l][c * 128:(c + 1) * 128]
    rpb = inp["na_rpb"]
    kcol = np.arange(64)[:, None]
    qcol = np.arange(64)[None, :]
    ci = kcol - qcol + 15
    ok = (ci >= 0) & (ci <= 30)
    cic = np.clip(ci, 0, 30)
    nab = np.zeros((DEPTH, 2, 64, 2, 15, 64), np.float32)
    for l in range(DEPTH):
        for pr in range(2):
            for a in range(2):
                for t in range(15):
                    g = rpb[l, 2 * pr + a, 14 - t][cic]
                    nab[l, a, :, pr, t, :] = np.where(ok, g, np.float32(0.0))
    nab = nab.reshape(DEPTH, 128, 2 * 15 * 64)
    lbraw = np.ascontiguousarray(inp["hg_lower_bounds"].reshape(DEPTH * 2 * 4, 128).T).astype(np.float32)
    fng = np.ascontiguousarray(inp["final_norm_g"].reshape(1, D)).astype(np.float32)
    return cols, nab, lbraw, fng


_NC_CACHE = {}


def _get_nc(stage="full", dbg=None):
    key = (stage, dbg)
    if key not in _NC_CACHE:
        _NC_CACHE[key] = build_program(stage, dbg)
    return _NC_CACHE[key]


def make_in_maps(inp, ncores):
    kfc, kbc, colneg = _consts()
    cols, nab, lbraw, fng = _layout_small(inp)
    f = lambda a: np.ascontiguousarray(np.asarray(a, dtype=np.float32))
    shared = dict(w_in=f(inp["w_in"]), w_out=f(inp["w_out"]), w_gate_up=f(inp["w_gate_up"]), w_down=f(inp["w_down"]),
                  cols=cols, kf=kfc, kb=kbc, lbraw=lbraw, fng=fng, nab=nab, colneg=colneg)
    x = f(inp["x"])
    return [dict(shared, x=x[i]) for i in range(ncores)]


def kernel(**inputs):
    nc = _get_nc("full", None)
    in_maps = make_in_maps(inputs, 8)
    res = run_bass_kernel_spmd(nc, in_maps, core_ids=list(range(8)))
    return np.stack([np.asarray(r["out"], dtype=np.float32) for r in res.results], axis=0)
```

```python
import numpy as np
import concourse.bass as bass
import concourse.mybir as mybir
from concourse.bass_utils import run_bass_kernel_spmd

F32 = mybir.dt.float32
BF16 = mybir.dt.bfloat16
AF = mybir.ActivationFunctionType
ALU = mybir.AluOpType

T = 2048
D = 1024
KC = 8
DIN = 3840
FFN = 2816
NJ = FFN // 128
DEPTH = 2
EPS = 1e-6
NEGM = -30000.0
HCH = 64
NCH = 128 // HCH

C_MIXG = 0
C_FFNG = 16
C_HGN = 32
C_CONVW = 34
C_CONVB = 158
C_LNG = 162
C_LNB = 166
NCOLS = 170

K_IDF = 0
K_UF = 128
K_UB = 256
K_MF = 384
K_MB = 512
K_SCM = 128
NKF = 640
B_ID = 0
B_ONES = 128
B_OBD = 256
B_ZERO = 384
B_UF = 512
B_UB = 640
B_NUF = 768
B_NUB = 896
NKB = 1024


class Prog:
    ENGS = ("pe", "act", "dve", "pool", "sp")

    def __init__(self, nc):
        self.nc = nc
        self.ops = []
        self.lastw = {}
        self.readers = {}
        self.dma_cnt = {}
        self.final_waits = []
        self.last_on = {}
        self.bar = {}

    def barrier(self):
        snap = dict(self.last_on)
        for e in self.ENGS:
            self.bar[e] = set(snap.values())

    def op(self, eng, fn, reads=(), writes=(), dma_slot=None, nobar=False, cost=0.3):
        idx = len(self.ops)
        deps = set()
        if not nobar and self.bar.get(eng):
            deps |= self.bar[eng]
        for t in reads:
            w = self.lastw.get(t)
            if w is not None:
                deps.add(w)
        for t in writes:
            w = self.lastw.get(t)
            if w is not None:
                deps.add(w)
            for r in self.readers.get(t, ()):
                deps.add(r)
        rec = dict(eng=eng, fn=fn, deps=deps, dma_slot=dma_slot, dma_val=None, rawdeps=set(), cost=cost)
        for t in reads:
            w = self.lastw.get(t)
            if w is not None:
                rec["rawdeps"].add(w)
        if dma_slot is not None:
            self.dma_cnt[dma_slot] = self.dma_cnt.get(dma_slot, 0) + 1
            rec["dma_val"] = 16 * self.dma_cnt[dma_slot]
        self.ops.append(rec)
        if dma_slot is None:
            self.last_on[eng] = idx
        for t in reads:
            self.readers.setdefault(t, []).append(idx)
        for t in writes:
            self.lastw[t] = idx
            self.readers[t] = []
        return idx

    def schedule(self):
        import heapq
        ops = self.ops
        n = len(ops)
        children = [[] for _ in range(n)]
        indeg = [0] * n
        for i, o in enumerate(ops):
            o["alldeps"] = set(o["deps"])
            indeg[i] = len(o["alldeps"])
            for d in o["alldeps"]:
                children[d].append(i)
        finish = [0.0] * n
        ready = [0.0] * n
        blevel = [0.0] * n
        for i in range(n - 1, -1, -1):
            m = 0.0
            for c in children[i]:
                if blevel[c] > m:
                    m = blevel[c]
            blevel[i] = m + ops[i]["cost"] + 0.4
        prio = [(-blevel[i], i) for i in range(n)]
        tfree = {e: 0.0 for e in self.ENGS}
        dma_free = 0.0
        timeheap = {e: [] for e in self.ENGS}
        idxheap = {e: [] for e in self.ENGS}
        order = {e: [] for e in self.ENGS}
        for i, o in enumerate(ops):
            if indeg[i] == 0:
                heapq.heappush(timeheap[o["eng"]], (0.0, i))
        done = 0
        while done < n:
            best = None
            for e in self.ENGS:
                th, ih = timeheap[e], idxheap[e]
                while th and th[0][0] <= tfree[e]:
                    heapq.heappush(ih, prio[heapq.heappop(th)[1]])
                if ih:
                    cand = (tfree[e], ih[0][1], e, True)
                elif th:
                    cand = (th[0][0], th[0][1], e, False)
                else:
                    continue
                if best is None or cand[:2] < best[:2]:
                    best = cand
            start, i, e, fromidx = best
            if fromidx:
                heapq.heappop(idxheap[e])
            else:
                heapq.heappop(timeheap[e])
            o = ops[i]
            if o["dma_slot"] is not None:
                tfree[e] = start + (1.05 if e == "pool" else 0.1)
                t0 = max(start, dma_free)
                dma_free = t0 + o["cost"]
                finish[i] = dma_free + 2.0
            else:
                tfree[e] = start + o["cost"]
                finish[i] = tfree[e]
            order[e].append(i)
            done += 1
            for c in children[i]:
                hop = 0.05 if ops[c]["eng"] == e and o["dma_slot"] is None else 0.2
                r = finish[i] + hop
                if r > ready[c]:
                    ready[c] = r
                indeg[c] -= 1
                if indeg[c] == 0:
                    heapq.heappush(timeheap[ops[c]["eng"]], (ready[c], c))
        self.order = order
        self.est_time = max(finish) if n else 0.0

    def emit(self, final_ops):
        nc = self.nc
        ops = self.ops
        self.schedule()
        needed = [False] * len(ops)
        for i, o in enumerate(ops):
            keep = set()
            for d in o["deps"]:
                od = ops[d]
                if od["dma_slot"] is not None:
                    if o["dma_slot"] == od["dma_slot"] and d not in o["rawdeps"]:
                        continue
                    keep.add(d)
                    continue
                if od["eng"] == o["eng"] and o["dma_slot"] is None:
                    if o["eng"] == "pe":
                        continue
                keep.add(d)
            o["deps"] = keep
            for d in keep:
                needed[d] = True
        for d in final_ops:
            needed[d] = True
        cnt = {e: 0 for e in self.ENGS}
        dcnt = {}
        for e in self.ENGS:
            for i in self.order[e]:
                o = ops[i]
                if o["dma_slot"] is not None:
                    dcnt[o["dma_slot"]] = dcnt.get(o["dma_slot"], 0) + 1
                    o["ev"] = ("dma:" + o["dma_slot"], 16 * dcnt[o["dma_slot"]])
                else:
                    if needed[i]:
                        cnt[e] += 1
                        o["sig"] = True
                    else:
                        o["sig"] = False
                    o["ev"] = (e, cnt[e] if needed[i] else None)
        semnames = list(self.ENGS) + ["dma:" + s for s in self.dma_cnt]
        from contextlib import ExitStack
        with ExitStack() as st:
            sems = {}
            for sn in semnames:
                sems[sn] = st.enter_context(nc.semaphore("s_" + sn.replace(":", "_")))
            block = st.enter_context(nc.Block())
            per_eng = self.order

            def run(eng_name, handle):
                seen = {}
                for i in per_eng[eng_name]:
                    o = ops[i]
                    w = {}
                    for d in o["deps"]:
                        sk, v = ops[d]["ev"]
                        assert v is not None
                        if v > w.get(sk, 0):
                            w[sk] = v
                    for sk, v in w.items():
                        if seen.get(sk, 0) >= v:
                            continue
                        handle.wait_ge(sems[sk], v)
                        seen[sk] = v
                    ins = o["fn"](handle)
                    if o["dma_slot"] is not None:
                        ins.then_inc(sems["dma:" + o["dma_slot"]], 16)
                    elif o["sig"]:
                        ins.then_inc(sems[eng_name], 1)
                if eng_name == "sp":
                    w = {}
                    for d in final_ops:
                        sk, v = ops[d]["ev"]
                        if v > w.get(sk, 0):
                            w[sk] = v
                    for sk, v in w.items():
                        handle.wait_ge(sems[sk], v)

            @block.tensor
            def _(e):
                run("pe", e)

            @block.scalar
            def _(e):
                run("act", e)

            @block.vector
            def _(e):
                run("dve", e)

            @block.gpsimd
            def _(e):
                run("pool", e)

            @block.sync
            def _(e):
                run("sp", e)


def mk_ap(base, extra_off, dims):
    return bass.AP(tensor=base.tensor, offset=base.offset + extra_off,
                   ap=[list(base.ap[0])] + [list(d) for d in dims])


def build_program(stage="full", dbg=None):
    nc = bass.Bass("TRN2", target_bir_lowering=False)
    P = Prog(nc)
    x_d = nc.dram_tensor("x", [T, D], F32, kind="ExternalInput").ap()
    w_in_d = nc.dram_tensor("w_in", [DEPTH, D, DIN], F32, kind="ExternalInput").ap()
    w_out_d = nc.dram_tensor("w_out", [DEPTH, D, D], F32, kind="ExternalInput").ap()
    w_gu_d = nc.dram_tensor("w_gate_up", [DEPTH, D, 2 * FFN], F32, kind="ExternalInput").ap()
    w_dn_d = nc.dram_tensor("w_down", [DEPTH, FFN, D], F32, kind="ExternalInput").ap()
    cols_d = nc.dram_tensor("cols", [128, NCOLS], F32, kind="ExternalInput").ap()
    kf_d = nc.dram_tensor("kf", [128, NKF], F32, kind="ExternalInput").ap()
    kb_d = nc.dram_tensor("kb", [128, NKB], F32, kind="ExternalInput").ap()
    lbraw_d = nc.dram_tensor("lbraw", [128, 16], F32, kind="ExternalInput").ap()
    fng_d = nc.dram_tensor("fng", [1, D], F32, kind="ExternalInput").ap()
    nab_d = nc.dram_tensor("nab", [DEPTH, 128, 2 * 15 * 64], F32, kind="ExternalInput").ap()
    colneg_d = nc.dram_tensor("colneg", [128, 64], F32, kind="ExternalInput").ap()
    out_d = nc.dram_tensor("out", [T, D], F32, kind="ExternalOutput").ap()
    dbg_d = None
    if dbg is not None:
        dbg_d = nc.dram_tensor("dbg", [128, dbg[1]], dbg[2], kind="ExternalOutput").ap()

    from contextlib import ExitStack
    with ExitStack() as st:
        def sb(name, shape, dt):
            return st.enter_context(nc.sbuf_tensor(name, shape, dt))

        def ps(name):
            return st.enter_context(nc.psum_tensor(name, [128, 512], F32))

        xT = sb("xT", [128, KC, T], F32)
        hT = sb("hT", [128, KC, T], BF16)
        BIG = sb("BIG", [128, 41984], BF16)
        WB = [sb("WB%d" % i, [128, 6144], BF16) for i in range(2)]
        cols = sb("cols_sb", [128, NCOLS], F32)
        kf = sb("kf_sb", [128, NKF], F32)
        kb = sb("kb_sb", [128, NKB], BF16)
        lbc = sb("lbc", [128, 32], F32)
        PS = [ps("ps%d" % i) for i in range(8)]

        mixT = BIG[:, 0:KC * T].rearrange("p (k t) -> p k t", k=KC)
        SCR = BIG[:, KC * T:41984]
        SCRF = SCR.bitcast(F32)

        def colap(c):
            return cols[:, c:c + 1]

        def fsz(ap):
            n = 1
            for d in ap.shape[1:]:
                n *= int(d)
            return n

        PAGE = 256

        def pages(*aps):
            out = []
            for ap in aps:
                if ap is None or isinstance(ap, (int, float)):
                    continue
                if ap.tensor.name != "BIG":
                    continue
                esz = mybir.dt.size(ap.dtype)
                dims = [list(d) for d in ap.ap]
                pstep = int(dims[0][0])
                lo = (int(ap.offset) % pstep) * esz
                ext = 0
                for st, cnt in dims[1:]:
                    ext += (int(cnt) - 1) * abs(int(st))
                hi = lo + (ext + 1) * esz
                for k in range(lo // PAGE, (hi - 1) // PAGE + 1):
                    out.append(("pg", k))
            return out

        def ecost(eng, n):
            if eng == "act":
                return 0.2 + n * 0.00084
            if eng == "dve":
                return 0.16 + n * 0.00104
            return 0.3 + n * 0.002

        def dma(q, out, in_, slot, reads=(), writes=(), nobar=False):
            nbytes = fsz(out) * int(out.shape[0]) * 4
            return P.op(q, lambda e: e.dma_start(out=out, in_=in_), list(reads) + pages(in_), list(writes) + pages(out),
                        dma_slot=slot, nobar=nobar, cost=nbytes / 330e3)

        def mm(out, lhsT, rhs, start, stop, reads, writes, **kw):
            return P.op("pe", lambda e: e.matmul(out, lhsT, rhs, start=start, stop=stop, **kw),
                        list(reads) + pages(lhsT, rhs), writes, cost=0.03 + fsz(rhs) / 2400.0)

        def tr(out, in_, ident, reads, writes):
            return P.op("pe", lambda e: e.transpose(out, in_, ident), list(reads) + pages(in_), writes, cost=0.12)

        def act(out, in_, func, reads, writes, scale=1.0, bias=0.0, accum_out=None):
            kw = {}
            if accum_out is not None:
                kw["accum_out"] = accum_out
            return P.op("act", lambda e: e.activation(out=out, in_=in_, func=func, scale=scale, bias=bias, **kw),
                        list(reads) + pages(in_, bias if not isinstance(bias, float) else None),
                        list(writes) + pages(out, accum_out), cost=ecost("act", fsz(out)))

        def tt(eng, out, in0, in1, op, reads, writes):
            return P.op(eng, lambda e: e.tensor_tensor(out=out, in0=in0, in1=in1, op=op),
                        list(reads) + pages(in0, in1), list(writes) + pages(out), cost=ecost(eng, fsz(out)))

        def ts(eng, out, in0, s1, op0, reads, writes, s2=None, op1=None):
            if op1 is None:
                return P.op(eng, lambda e: e.tensor_scalar(out=out, in0=in0, scalar1=s1, scalar2=None, op0=op0),
                            list(reads) + pages(in0, s1), list(writes) + pages(out), cost=ecost(eng, fsz(out)))
            return P.op(eng, lambda e: e.tensor_scalar(out=out, in0=in0, scalar1=s1, scalar2=s2, op0=op0, op1=op1),
                        list(reads) + pages(in0, s1, s2), list(writes) + pages(out), cost=ecost(eng, fsz(out)))

        def stt(out, in0, scalar, in1, op0, op1, reads, writes):
            return P.op("dve", lambda e: e.scalar_tensor_tensor(out=out, in0=in0, scalar=scalar, in1=in1,
                                                                 op0=op0, op1=op1),
                        list(reads) + pages(in0, scalar, in1), list(writes) + pages(out),
                        cost=ecost("dve", fsz(out)) + 0.06)

        def cp(eng, out, in_, reads, writes):
            if eng == "act":
                return P.op("act", lambda e: e.copy(out=out, in_=in_), list(reads) + pages(in_),
                            list(writes) + pages(out), cost=ecost("act", fsz(out)))
            return P.op(eng, lambda e: e.tensor_copy(out=out, in_=in_), list(reads) + pages(in_),
                        list(writes) + pages(out), cost=ecost(eng, fsz(out)))

        wslot_ctr = [0]

        def wslot():
            i = wslot_ctr[0] % 2
            wslot_ctr[0] += 1
            return i

        dma("sp", cols[:], cols_d, "c_cols", (), ["cols"])
        dma("sp", kf[:], kf_d, "c_kf", (), ["kf"])
        kbs = SCRF[:, 6656:6656 + NKB]
        dma("sp", kbs, kb_d, "c_kb", (), ["kbs"])
        cp("dve", kb[:], kbs, ["kbs"], ["kb"])
        ident_f = kf[:, K_IDF:K_IDF + 128]
        ident_b = kb[:, B_ID:B_ID + 128]
        ones_b = kb[:, B_ONES:B_ONES + 128]

        xin = SCRF[:, 0:4096].rearrange("p (a d) -> p a d", a=4)
        for tile in range(16):
            s = tile % 4
            dma("sp", xin[:, s, :], x_d[tile * 128:(tile + 1) * 128, :], "xin%d" % s, (), [("xin", s)])
            for half in range(2):
                bank = PS[(tile * 2 + half) % 4]
                bname = "ps%d" % ((tile * 2 + half) % 4)
                for j in range(4):
                    kc = half * 4 + j
                    tr(bank[:, j * 128:(j + 1) * 128], xin[:, s, kc * 128:(kc + 1) * 128], ident_f,
                       [("xin", s), "kf"], [bname])
                eng = "dve" if half == 0 else "act"
                cp(eng, xT[:, half * 4:half * 4 + 4, tile * 128:(tile + 1) * 128],
                   bank[:].rearrange("p (j c) -> p j c", j=4), [bname],
                   [("xT", kc, tile // 4) for kc in range(half * 4, half * 4 + 4)])

        def rmsnorm(gcol0, tag):
            sq = SCR[:, 8192:12288].rearrange("p (k t) -> p k t", k=KC)
            rstd = SCRF[:, 6144:6656]
            for blk in range(4):
                tsl = slice(blk * 512, (blk + 1) * 512)
                for kc in range(KC):
                    act(sq[:, kc, :], xT[:, kc, tsl], AF.Square, [("xT", kc, blk)], [("sq", kc)])
                for kc in range(KC):
                    mm(PS[4][:], ones_b, sq[:, kc, :], kc == 0, kc == KC - 1, [("sq", kc), "kb"], ["ps4"])
                act(rstd, PS[4][:], AF.Ln, ["ps4"], ["rstd"], scale=1.0 / D, bias=EPS)
                act(rstd, rstd, AF.Exp, ["rstd"], ["rstd"], scale=-0.5)
                for kc in range(KC):
                    stt(hT[:, kc, tsl], xT[:, kc, tsl], colap(gcol0 + kc), rstd, ALU.mult, ALU.mult,
                        [("xT", kc, blk), "rstd", "cols"], [("hT", kc, blk)])

        def load_w_in(layer, groups, after=()):
            si = wslot()
            slot = WB[si]
            aps = []
            off = 0
            wv = w_in_d[layer].rearrange("(kc p) n -> p kc n", p=128)
            for (c0, n) in groups:
                ap = slot[:, off:off + KC * n].rearrange("p (k n) -> p k n", k=KC)
                dma("pool", ap, wv[:, :, c0:c0 + n], "wb%d" % si, list(after), [("WB", si)], nobar=True)
                aps.append(ap)
                off += KC * n
            return si, aps

        def proj_fm(bank, bname, w_ap, si, blk, c0=0, n=128):
            for kc in range(KC):
                mm(bank[0:n, :], w_ap[:, kc, c0:c0 + n], hT[:, kc, blk * 512:(blk + 1) * 512], kc == 0, kc == KC - 1,
                   [("WB", si), ("hT", kc, blk)], [bname])

        def lower_bounds():
            dma("sp", lbc[:, 0:16], lbraw_d, "c_lb", (), ["lbc"])
            act(lbc[:, 0:16], lbc[:, 0:16], AF.Exp, ["lbc"], ["lbc"])
            tt("dve", lbc[:, 24:32], lbc[:, 0:8], lbc[:, 8:16], ALU.add, ["lbc"], ["lbc"])
            P.op("dve", lambda e: e.reciprocal(out=lbc[:, 24:32], in_=lbc[:, 24:32]), ["lbc"], ["lbc"])
            tt("dve", lbc[:, 16:24], lbc[:, 8:16], lbc[:, 24:32], ALU.mult, ["lbc"], ["lbc"])
            ts("dve", lbc[:, 24:32], lbc[:, 16:24], -1.0, ALU.mult, ["lbc"], ["lbc"], s2=1.0, op1=ALU.add)

        def conv_mixer(layer):
            CH = 2
            ub = SCR[:, 0:CH * 2080].rearrange("p (c t) -> p c t", c=CH)
            dg = SCR[:, 4160:4160 + 62 * 128].rearrange("p (c j k) -> p c j k", c=CH, j=31)
            FB = 6080
            a2 = SCRF[:, FB:FB + 1024].rearrange("p (c t) -> p c t", c=CH)
            tmp = SCRF[:, FB + 1024:FB + 1536]
            tmp2 = SCRF[:, FB + 1536:FB + 2048]
            sg = SCRF[:, FB + 2048:FB + 2560]
            sqb = SCR[:, 2 * (FB + 2560):2 * (FB + 2560) + 1024].rearrange("p (c t) -> p c t", c=CH)
            accb = SCR[:, 2 * (FB + 2560) + 1024:2 * (FB + 2560) + 2048].rearrange("p (c t) -> p c t", c=CH)
            si, (wa, wg) = load_w_in(layer, [(3328, 256), (3584, 256)],
                                     after=[("xin", i) for i in range(4)] if layer == 0 else ())
            U = lambda c: [("u", c, b) for b in (-1, 0, 1, 2, 3, 4)]
            for c in range(CH):
                P.op("pool", lambda e, c=c: e.memset(ub[:, c, 0:16], 0.0), (), [("u", c, -1)] + pages(ub[:, c, 0:16]))
                P.op("pool", lambda e, c=c: e.memset(ub[:, c, 16 + T:2080], 0.0), (),
                     [("u", c, 4)] + pages(ub[:, c, 16 + T:2080]))
                wc0 = C_CONVW + (layer * 2 + c) * 31
                tt("dve", dg[:, c, :, :], mk_ap(ident_b, 0, [[0, 31], [1, 128]]),
                   mk_ap(cols[:, wc0:wc0 + 31], 0, [[1, 31], [0, 128]]), ALU.mult, ["kb", "cols"], [("dg", c)])
            for c in range(CH):
                for blk in range(4):
                    proj_fm(PS[0], "ps0", wa, si, blk, c * 128)
                    proj_fm(PS[1], "ps1", wg, si, blk, c * 128)
                    act(tmp, PS[1][:], AF.Exp, ["ps1"], ["ctmp"], scale=-1.0)
                    act(tmp, tmp, AF.Ln, ["ctmp"], ["ctmp"], bias=1.0)
                    act(tmp, tmp, AF.Exp, ["ctmp"], ["ctmp"], scale=-1.0)
                    tt("dve", ub[:, c, 16 + blk * 512:16 + (blk + 1) * 512], PS[0][:], tmp, ALU.mult,
                       ["ps0", "ctmp"], [("u", c, blk)])
            for blk in range(4):
                tsl = slice(blk * 512, (blk + 1) * 512)
                for c in range(CH):
                    bank, bname = PS[2 + c], "ps%d" % (2 + c)
                    for j in range(31):
                        mm(bank[:], dg[:, c, j, :], ub[:, c, blk * 512 + j + 1:blk * 512 + j + 1 + 512], j == 0, j == 30,
                           [("dg", c)] + [("u", c, b) for b in (blk - 1, blk, blk + 1)], [bname])
                    act(a2[:, c, :], bank[:], AF.Identity, [bname, "cols"], [("a2", c)],
                        bias=colap(C_CONVB + layer * 2 + c))
                for c in range(CH):
                    act(sqb[:, c, :], a2[:, c, :], AF.Square, [("a2", c)], [("csq", c)])
                    cp("dve", accb[:, c, :], a2[:, c, :], [("a2", c)], [("cab", c)])
                for c in range(CH):
                    mm(PS[4][:], ones_b, accb[:, c, :], c == 0, c == CH - 1, [("cab", c), "kb"], ["ps4"])
                for c in range(CH):
                    mm(PS[5][:], ones_b, sqb[:, c, :], c == 0, c == CH - 1, [("csq", c), "kb"], ["ps5"])
                ts("dve", tmp, PS[4][:], 1.0 / 256, ALU.mult, ["ps4"], ["ctmp"])
                tt("dve", tmp2, tmp, tmp, ALU.mult, ["ctmp"], ["ctmp2"])
                stt(tmp2, PS[5][:], 1.0 / 256, tmp2, ALU.mult, ALU.subtract, ["ps5", "ctmp2"], ["ctmp2"])
                act(tmp2, tmp2, AF.Ln, ["ctmp2"], ["ctmp2"], bias=EPS)
                act(tmp2, tmp2, AF.Exp, ["ctmp2"], ["ctmp2"], scale=-0.5)
                for c in range(CH):
                    a = a2[:, c, :]
                    A = ("a2", c)
                    tt("dve", a, a, tmp, ALU.subtract, [A, "ctmp"], [A])
                    tt("dve", a, a, tmp2, ALU.mult, [A, "ctmp2"], [A])
                    ts("dve", a, a, colap(C_LNG + layer * 2 + c), ALU.mult, [A, "cols"], [A],
                       s2=colap(C_LNB + layer * 2 + c), op1=ALU.add)
                    act(sg, a, AF.Exp, [A], ["csg"], scale=-1.0)
                    act(sg, sg, AF.Ln, ["csg"], ["csg"], bias=1.0)
                    act(sg, sg, AF.Exp, ["csg"], ["csg"], scale=-1.0)
                    tt("dve", mixT[:, 6 + c, tsl], a, sg, ALU.mult, [A, "csg"], [("mixT", 6 + c, blk)])

        def kr0(r):
            return min(max(r - 4, 0), 24)

        def na_mixer(layer):
            si, (wq, wk, wv) = load_w_in(layer, [(2560, 256), (2816, 256), (3072, 256)],
                                         after=[("xin", i) for i in range(4)] if layer == 0 else ())
            qT = SCR[:, 0:2048]
            Kbd = SCR[:, 2048:6144].rearrange("p (r k) -> p r k", r=32)
            Vbd = SCR[:, 6144:10240].rearrange("p (r k) -> p r k", r=32)
            tblf = SCRF[:, 5120:7040].rearrange("p (a t q) -> p a t q", a=2, t=15)
            tblf3 = SCRF[:, 5120:7040].rearrange("p (a q) -> p a q", q=64)
            PT = [SCR[:, 14080 + i * 512:14080 + (i + 1) * 512] for i in range(2)]
            rec = SCRF[:, 7552:8064]
            TMP = [SCRF[:, 8064 + i * 512:8064 + (i + 1) * 512] for i in range(2)]
            cng = SCRF[:, 9088:9152]
            VTbd_flat = SCR[:, 18304:22400]
            VTbd = VTbd_flat.rearrange("p (r k) -> p r k", r=32)
            dma("sp", SCRF[:, 5120:7040], nab_d[layer], "c_nab", (), ["tblf"])
            dma("sp", cng, colneg_d, "c_cng", (), ["cng"])
            tt("dve", tblf3, tblf3, mk_ap(cng, 0, [[0, 30], [1, 64]]), ALU.add, ["tblf", "cng"], ["tblf"])
            obd = kb[:, B_OBD:B_OBD + 128]
            zer = kb[:, B_ZERO:B_ZERO + 128]
            P.op("pool", lambda e: e.memset(SCR[:, 2048:6144], 0.0), (), ["Kbd"] + pages(SCR[:, 2048:6144]), cost=3.5)
            P.op("pool", lambda e: e.memset(VTbd_flat, 0.0), (), ["VTbd"] + pages(VTbd_flat), cost=3.5)
            for pr in range(2):
                for blk in range(4):
                    tsl = slice(blk * 512, (blk + 1) * 512)
                    proj_fm(PS[0], "ps0", wq, si, blk, pr * 128)
                    cp("act", qT[:, tsl], PS[0][:], ["ps0"], [("qT", blk)])
                    proj_fm(PS[1], "ps1", wk, si, blk, pr * 128)
                    for a_ in range(2):
                        cp("dve" if a_ == 0 else "act",
                           Kbd[a_ * 64:(a_ + 1) * 64, blk * 8:(blk + 1) * 8, a_ * 64:(a_ + 1) * 64],
                           PS[1][a_ * 64:(a_ + 1) * 64, :].rearrange("p (r k) -> p r k", r=8), ["ps1"], ["Kbd"])
                for blk in range(4):
                    proj_fm(PS[2], "ps2", wv, si, blk, pr * 128)
                    for a_ in range(2):
                        cp("dve" if a_ == 0 else "act",
                           VTbd[a_ * 64:(a_ + 1) * 64, blk * 8:(blk + 1) * 8, a_ * 64:(a_ + 1) * 64],
                           PS[2][a_ * 64:(a_ + 1) * 64, :].rearrange("p (r k) -> p r k", r=8), ["ps2"], ["VTbd"])
                for kr8 in range(4):
                    bank, bname = PS[3], "ps3"
                    bankb = bank[:].bitcast(BF16)
                    for i8 in range(8):
                        kr = kr8 * 8 + i8
                        tr(bankb[:, i8 * 128:(i8 + 1) * 128], VTbd[:, kr, :], ident_b, ["VTbd", "kb"], [bname])
                    cp("dve" if kr8 % 2 == 0 else "act", Vbd[:, kr8 * 8:(kr8 + 1) * 8, :],
                       bankb.rearrange("p (r k) -> p r k", r=8), [bname], ["Vbd"])
                unit = 0
                for g in range(4):
                    r0 = 8 * g
                    Ob, Obn = PS[4 + (g % 2) * 2], "ps%d" % (4 + (g % 2) * 2)
                    Db, Dbn = PS[5 + (g % 2) * 2], "ps%d" % (5 + (g % 2) * 2)
                    mm(Ob[:], zer, qT[:, 0:512], True, False, ["kb", ("qT", 0)], [Obn])
                    mm(Db[:], zer, qT[:, 0:512], True, False, ["kb", ("qT", 0)], [Dbn])
                    krs = list(range(kr0(r0), kr0(r0 + 7) + 8))
                    for ki, kr in enumerate(krs):
                        rows = [r for r in range(r0, r0 + 8) if kr0(r) <= kr <= kr0(r) + 7]
                        ra, rb = rows[0], rows[-1]
                        n = (rb - ra + 1) * 64
                        t0 = ra - kr + 7
                        c0 = (ra - r0) * 64
                        sbi = unit % 2
                        unit += 1
                        Sb_, Sbn = PS[sbi], "ps%d" % sbi
                        pt, tmp = PT[sbi], TMP[sbi]
                        qdeps = sorted(set([("qT", (ra * 64) // 512), ("qT", (rb * 64 + 63) // 512)]))
                        mm(Sb_[:, 0:n], Kbd[:, kr, :], qT[:, ra * 64:ra * 64 + n], True, True, ["Kbd"] + qdeps, [Sbn])
                        stt(tmp[:, 0:n], Sb_[:, 0:n], 0.125,
                            tblf[:, pr, t0:t0 + (rb - ra + 1), :].rearrange("p t q -> p (t q)"),
                            ALU.mult, ALU.add, [Sbn, "tblf"], [("natmp", sbi)])
                        act(pt[:, 0:n], tmp[:, 0:n], AF.Exp, [("natmp", sbi)], [("PT", sbi)])
                        last = ki == len(krs) - 1
                        mm(Ob[:, c0:c0 + n], Vbd[:, kr, :], pt[:, 0:n], False, last, ["Vbd", ("PT", sbi)], [Obn])
                        mm(Db[:, c0:c0 + n], obd, pt[:, 0:n], False, last, ["kb", ("PT", sbi)], [Dbn])
                    act(rec, Db[:], AF.Ln, [Dbn], ["narec"])
                    act(rec, rec, AF.Exp, ["narec"], ["narec"], scale=-1.0)
                    tt("dve", mixT[:, 4 + pr, r0 * 64:r0 * 64 + 512], Ob[:], rec, ALU.mult, [Obn, "narec"],
                       [("mixT", 4 + pr, g)])

        def hgrn2_head(layer, h):
            si, (wq, wff, wfb, wi, wg) = load_w_in(layer, [(h * 128, 128), (512 + h * 128, 128), (1024 + h * 128, 128),
                                                           (1536 + h * 128, 128), (2048 + h * 128, 128)])
            W = ("WB", si)
            sqT = SCR[:, 0:2048]
            vtm = SCR[:, 2048:4096].rearrange("p (t d) -> p t d", t=16)
            oacc = SCRF[:, 2048:4096]
            AE = []
            for i in range(2):
                b0 = 4096 + i * 2560
                AE.append([SCRF[:, b0 + j * 512:b0 + (j + 1) * 512] for j in range(5)])
            kin = SCR[:, 2 * 9216:2 * 9216 + 512]
            kkT = SCR[:, 2 * 9472:2 * 9472 + 512]
            HO = []
            for i in range(2):
                fb = 9728 + i * 1088
                HO.append(dict(qin=SCR[:, 2 * fb:2 * fb + 512], PTm=SCR[:, 2 * fb + 512:2 * fb + 1024],
                               kkx=SCR[:, 2 * fb + 1024:2 * fb + 1024 + 512 * NCH].rearrange("p (t c d) -> p t c d", t=4, c=NCH),
                               kkx_flat=SCR[:, 2 * fb + 1024:2 * fb + 1024 + 512 * NCH],
                               dec=SCRF[:, fb + 1024:fb + 1040], i=i))
            X = [SCRF[:, 11904 + i * 256:11904 + (i + 1) * 256] for i in range(2)]
            Sbb = [SCR[:, 2 * 12416 + i * 256:2 * 12416 + (i + 1) * 256] for i in range(2)]
            zer = kb[:, B_ZERO:B_ZERO + 128]
            scm = kf[:, K_SCM:K_SCM + 512]
            hgn = colap(C_HGN + layer)
            for blk in range(4):
                tsl = slice(blk * 512, (blk + 1) * 512)
                proj_fm(PS[6], "ps6", wq, si, blk)
                A_ = AE[blk % 2][0]
                An = "A%d" % (blk % 2)
                act(A_, PS[6][:], AF.Exp, ["ps6"], [An], scale=-1.0)
                act(A_, A_, AF.Ln, [An], [An], bias=1.0)
                act(A_, A_, AF.Exp, [An], [An], scale=-1.0)
                tt("dve", sqT[:, tsl], PS[6][:], A_, ALU.mult, ["ps6", An], [("sqT", blk)])
                for t4 in range(4):
                    tile = blk * 4 + t4
                    for kc in range(KC):
                        mm(PS[7][:, t4 * 128:(t4 + 1) * 128], hT[:, kc, tile * 128:(tile + 1) * 128], wi[:, kc, :],
                           kc == 0, kc == KC - 1, [W, ("hT", kc, blk)], ["ps7"])
                cp("dve", vtm[:, blk * 4:(blk + 1) * 4, :], PS[7][:].rearrange("p (t d) -> p t d", t=4), ["ps7"],
                   [("vtm", blk)])
            for ho in HO:
                P.op("pool", lambda e, ho=ho: e.memset(ho["kkx_flat"], 0.0), (), [("kkx", ho["i"])] + pages(ho["kkx_flat"]),
                     cost=1.0)

            touched = set()
            state = dict(xi=0, prevSb=None)

            def front(u):
                dirn, blk, ho = u["dirn"], u["blk"], HO[u["hi"]]
                hi = u["hi"]
                tsl = slice(blk * 512, (blk + 1) * 512)
                wf_ = wfb if dirn else wff
                A_, B_, C_, D_, E_ = AE[u["ae"]]
                nA, nB, nC, nD, nE = ["%s%d" % (x, u["ae"]) for x in "ABCDE"]
                proj_fm(PS[0], "ps0", wf_, si, blk)
                act(B_, PS[0][:], AF.Exp, ["ps0"], [nB], scale=-1.0)
                act(B_, B_, AF.Ln, [nB], [nB], bias=1.0)
                act(A_, B_, AF.Exp, [nB], [nA], scale=-1.0)
                if layer == 0:
                    sg = -1.0
                else:
                    ci = dirn * 4 + h
                    ts("dve", A_, A_, lbc[:, 24 + ci:25 + ci], ALU.mult, [nA, "lbc"], [nA],
                       s2=lbc[:, 16 + ci:17 + ci], op1=ALU.add)
                    act(B_, A_, AF.Ln, [nA], [nB])
                    sg = 1.0
                P.op("dve", lambda e: e.tensor_tensor_scan(out=C_, data0=scm, data1=B_, initial=0.0,
                                                          op0=ALU.mult, op1=ALU.add), [nB, "kf"] + pages(B_), [nC] + pages(C_), cost=1.25)
                blast = mk_ap(C_, HCH - 1, [[HCH, 512 // HCH], [0, HCH]])
                c3 = C_.rearrange("p (n c) -> p n c", c=HCH)
                d3 = D_.rearrange("p (n c) -> p n c", c=HCH)
                act(ho["dec"][:, 0:512 // HCH], mk_ap(C_, HCH - 1, [[HCH, 512 // HCH]]), AF.Exp, [nC], [("dec", hi)],
                    scale=sg)
                if dirn == 0:
                    act(E_, C_, AF.Exp, [nC], [nE], scale=-sg)
                    act(C_, C_, AF.Exp, [nC], [nC], scale=sg)
                    Eb, Enb = C_, E_
                    ebn, enbn = nC, nE
                else:
                    tt("dve", d3, blast, c3, ALU.subtract, [nC], [nD])
                    tt("dve", E_, B_, D_, ALU.add, [nB, nD], [nE])
                    act(C_, E_, AF.Exp, [nE], [nC], scale=-sg)
                    act(E_, E_, AF.Exp, [nE], [nE], scale=sg)
                    Eb, Enb = E_, C_
                    ebn, enbn = nE, nC
                tt("dve", ho["qin"], sqT[:, tsl], Eb, ALU.mult, [("sqT", blk), ebn], [("qin", hi)])
                stt(D_, A_, 1.0, Enb, ALU.subtract, ALU.mult, [nA, enbn], [nD])
                cp("dve", kin, D_, [nD], ["kin"])
                tt("dve", kkT.rearrange("p (n c) -> p n c", c=HCH), d3,
                   mk_ap(ho["dec"], 0, [[1, 512 // HCH], [0, HCH]]), ALU.mult, [nD, ("dec", hi)], ["kkT"])
                kb_ = PS[1][:].bitcast(BF16)
                for t4 in range(4):
                    cs = slice(t4 * 128, (t4 + 1) * 128)
                    tr(kb_[:, cs], kkT[:, cs], ident_b, ["kkT", "kb"], ["ps1"])
                for ch in range(NCH):
                    psl = slice(ch * HCH, (ch + 1) * HCH)
                    cp("act", ho["kkx"][psl, :, ch, :],
                       kb_[psl, 0:512].rearrange("p (t d) -> p t d", t=4), ["ps1"], [("kkx", hi)])
                for t4 in range(4):
                    cs = slice(t4 * 128, (t4 + 1) * 128)
                    mm(PS[2][:, cs], kin[:, cs], ho["qin"][:, cs], True, True, ["kin", ("qin", hi)], ["ps2"])
                Umask = kb[:, (B_NUB if dirn else B_NUF):(B_NUB if dirn else B_NUF) + 128]
                tt("dve", ho["PTm"].rearrange("p (t c) -> p t c", t=4), PS[2][:].rearrange("p (t c) -> p t c", t=4),
                   mk_ap(Umask, 0, [[0, 4], [1, 128]]), ALU.mult, ["ps2", "kb"], [("PTm", hi)])

            def back(u):
                dirn, blk, ho = u["dirn"], u["blk"], HO[u["hi"]]
                hi = u["hi"]
                A_, B_, C_, D_, E_ = AE[u["ae"]]
                nA, nB, nC, nD, nE = ["%s%d" % (x, u["ae"]) for x in "ABCDE"]
                tsl = slice(blk * 512, (blk + 1) * 512)
                order = [3, 2, 1, 0] if dirn else [0, 1, 2, 3]
                corder = list(range(NCH - 1, -1, -1)) if dirn else list(range(NCH))
                if u["first"]:
                    state["prevSb"] = None
                for t4 in order:
                    tile = blk * 4 + t4
                    cs = slice(t4 * 128, (t4 + 1) * 128)
                    dsb, dsn = PS[3 + state["xi"] % 2], "ps%d" % (3 + state["xi"] % 2)
                    xi = state["xi"] % 2
                    Xc, Xp = X[xi], X[1 - xi]
                    for ch in range(NCH):
                        mm(dsb[:, ch * 128:(ch + 1) * 128], ho["kkx"][:, t4, ch, :], vtm[:, tile, :], True, True,
                           [("kkx", hi), ("vtm", blk)], [dsn])
                    for n_, ch in enumerate(corder):
                        c0 = t4 * NCH + ch
                        dcol = ho["dec"][:, c0:c0 + 1]
                        xo = Xc[:, ch * 128:(ch + 1) * 128]
                        if n_ == 0:
                            if state["prevSb"] is None:
                                ts("dve", xo, dsb[:, ch * 128:(ch + 1) * 128], -1.0, ALU.mult, [dsn], [("X", xi)])
                            else:
                                lastch = corder[-1]
                                stt(xo, Xp[:, lastch * 128:(lastch + 1) * 128], dcol, dsb[:, ch * 128:(ch + 1) * 128],
                                    ALU.mult, ALU.subtract, [("X", 1 - xi), ("dec", hi), dsn], [("X", xi)])
                        else:
                            pch = corder[n_ - 1]
                            stt(xo, Xc[:, pch * 128:(pch + 1) * 128], dcol, dsb[:, ch * 128:(ch + 1) * 128],
                                ALU.mult, ALU.subtract, [("X", xi), ("dec", hi), dsn], [("X", xi)])
                    cp("act", Sbb[xi][:, 0:NCH * 128], Xc[:, 0:NCH * 128], [("X", xi)], [("Sb", xi)])
                    mm(PS[5][:, cs], vtm[:, tile, :], ho["PTm"][:, cs], True, False, [("vtm", blk), ("PTm", hi)], ["ps5"])
                    for n_, ch in enumerate(corder):
                        c0 = t4 * 128 + ch * HCH
                        if n_ == 0:
                            if state["prevSb"] is None:
                                lhs, ldep = zer, "kb"
                            else:
                                lastch = corder[-1]
                                lhs, ldep = Sbb[1 - xi][:, lastch * 128:(lastch + 1) * 128], ("Sb", 1 - xi)
                        else:
                            pch = corder[n_ - 1]
                            lhs, ldep = Sbb[xi][:, pch * 128:(pch + 1) * 128], ("Sb", xi)
                        mm(PS[5][:, c0:c0 + HCH], lhs, ho["qin"][:, c0:c0 + HCH], False, n_ == NCH - 1, [ldep, ("qin", hi)], ["ps5"])
                    state["prevSb"] = xi
                    state["xi"] += 1
                if blk not in touched:
                    touched.add(blk)
                    cp("act", oacc[:, tsl], PS[5][:], ["ps5"], [("oacc", blk)])
                else:
                    tt("dve", oacc[:, tsl], oacc[:, tsl], PS[5][:], ALU.add, ["ps5", ("oacc", blk)], [("oacc", blk)])
                    osq = C_.bitcast(BF16)[:, 0:512]
                    act(osq, oacc[:, tsl], AF.Square, [("oacc", blk)], [nC])
                    mm(PS[6][:], ones_b, osq, True, True, [nC, "kb"], ["ps6"])
                    act(B_, PS[6][:], AF.Ln, ["ps6"], [nB], scale=1.0 / 128, bias=EPS)
                    act(B_, B_, AF.Exp, [nB], [nB], scale=-0.5)
                    proj_fm(PS[7], "ps7", wg, si, blk)
                    act(A_, PS[7][:], AF.Exp, ["ps7"], [nA], scale=-1.0)
                    act(A_, A_, AF.Ln, [nA], [nA], bias=1.0)
                    act(A_, A_, AF.Exp, [nA], [nA], scale=-1.0)
                    tt("dve", A_, A_, PS[7][:], ALU.mult, [nA, "ps7"], [nA])
                    tt("dve", oacc[:, tsl], oacc[:, tsl], B_, ALU.mult, [("oacc", blk), nB], [("oacc", blk)])
                    stt(mixT[:, h, tsl], oacc[:, tsl], hgn, A_, ALU.mult, ALU.mult, [("oacc", blk), "cols", nA],
                        [("mixT", h, blk)])

            units = []
            for dirn in (1, 0):
                for n_, blk in enumerate([3, 2, 1, 0] if dirn else [0, 1, 2, 3]):
                    units.append(dict(dirn=dirn, blk=blk, first=(n_ == 0), hi=len(units) % 2, ae=len(units) % 2))
            front(units[0])
            for i, u in enumerate(units):
                if i + 1 < len(units):
                    front(units[i + 1])
                back(u)

        def out_proj(layer):
            wv = w_out_d[layer].rearrange("(kc p) n -> p kc n", p=128)
            waps = []
            for halfn in range(2):
                si = wslot()
                wap = WB[si][:, 0:KC * 512].rearrange("p (k n) -> p k n", k=KC)
                dma("pool", wap, wv[:, :, halfn * 512:(halfn + 1) * 512], "wb%d" % si, (), [("WB", si)], nobar=True)
                waps.append((si, wap))
            cnt = 0
            for blk in range(4):
                tsl = slice(blk * 512, (blk + 1) * 512)
                for fc in range(KC):
                    si, wap = waps[fc // 4]
                    fc4 = fc % 4
                    bi = cnt % 4
                    cnt += 1
                    bank, bname = PS[bi], "ps%d" % bi
                    for kc in range(KC):
                        mm(bank[:], wap[:, kc, fc4 * 128:(fc4 + 1) * 128], mixT[:, kc, tsl], kc == 0, kc == KC - 1,
                           [("WB", si), ("mixT", kc, blk)], [bname])
                    tt("dve", xT[:, fc, tsl], xT[:, fc, tsl], bank[:], ALU.add, [bname, ("xT", fc, blk)],
                       [("xT", fc, blk)])

        def ffn(layer):
            actT = BIG[:, 0:NJ * 1024].rearrange("p (j t) -> p j t", j=NJ)
            gv = w_gu_d[layer].rearrange("(kc p) n -> p kc n", p=128)
            dv = w_dn_d[layer].rearrange("(j p) n -> p j n", p=128)
            SIL = [SCRF[:, 9216 + i * 512:9216 + (i + 1) * 512] for i in range(2)]
            AT = [("BIGall",)]
            cnt = 0
            for half in range(2):
                for sl in range(NJ // 2):
                    si = wslot()
                    gap = WB[si][:, 0:2048].rearrange("p (k n) -> p k n", k=KC)
                    uap = WB[si][:, 2048:4096].rearrange("p (k n) -> p k n", k=KC)
                    dma("pool", gap, gv[:, :, sl * 256:(sl + 1) * 256], "wb%d" % si, (), [("WB", si)], nobar=True)
                    dma("pool", uap, gv[:, :, FFN + sl * 256:FFN + (sl + 1) * 256], "wb%d" % si, (), [("WB", si)], nobar=True)
                    for jj in range(2):
                        j = sl * 2 + jj
                        for b2 in range(2):
                            blk = half * 2 + b2
                            tsl = slice(blk * 512, (blk + 1) * 512)
                            bi = cnt % 2
                            cnt += 1
                            bg, bgn = PS[bi * 2], "ps%d" % (bi * 2)
                            bu, bun = PS[bi * 2 + 1], "ps%d" % (bi * 2 + 1)
                            for kc in range(KC):
                                mm(bg[:], gap[:, kc, jj * 128:(jj + 1) * 128], hT[:, kc, tsl], kc == 0, kc == KC - 1,
                                   [("WB", si), ("hT", kc, blk)], [bgn])
                            for kc in range(KC):
                                mm(bu[:], uap[:, kc, jj * 128:(jj + 1) * 128], hT[:, kc, tsl], kc == 0, kc == KC - 1,
                                   [("WB", si), ("hT", kc, blk)], [bun])
                            act(SIL[bi], bg[:], AF.Silu, [bgn], [("sil", bi)])
                            tt("dve", actT[:, j, b2 * 512:(b2 + 1) * 512], SIL[bi], bu[:], ALU.mult, [("sil", bi), bun],
                               [("actT", j, b2)] + mixall)
                for fp in range(4):
                    si = wslot()
                    dap = WB[si][:, 0:NJ * 256].rearrange("p (j n) -> p j n", j=NJ)
                    dma("pool", dap, dv[:, :, fp * 256:(fp + 1) * 256], "wb%d" % si, (), [("WB", si)], nobar=True)
                    for f2 in range(2):
                        fc = fp * 2 + f2
                        for b2 in range(2):
                            blk = half * 2 + b2
                            tsl = slice(blk * 512, (blk + 1) * 512)
                            bi = 4 + (cnt % 2)
                            cnt += 1
                            bank, bname = PS[bi], "ps%d" % bi
                            for j in range(NJ):
                                mm(bank[:], dap[:, j, f2 * 128:(f2 + 1) * 128], actT[:, j, b2 * 512:(b2 + 1) * 512],
                                   j == 0, j == NJ - 1, [("WB", si), ("actT", j, b2)], [bname])
                            tt("dve", xT[:, fc, tsl], xT[:, fc, tsl], bank[:], ALU.add, [bname, ("xT", fc, blk)],
                               [("xT", fc, blk)])

        mixall = [("mixT", kc, blk) for kc in range(KC) for blk in range(4)]

        def final_out():
            gbc = SCRF[:, 3072:4096]
            junk = [SCRF[:, 4096 + i * 512:4096 + (i + 1) * 512] for i in range(2)]
            ot = [SCRF[:, 5120 + i * 1024:5120 + (i + 1) * 1024] for i in range(4)]
            ssall = SCRF[:, 12288:12304]
            dma("sp", gbc, fng_d[0].partition_broadcast(128), "c3", (), ["gbc"])
            fin = []
            for tile in range(16):
                o = tile % 4
                ss = ssall[:, o * 4:o * 4 + 4]
                bp = 3 if tile < 8 else o
                banks = [(PS[bp * 2 + hf], "ps%d" % (bp * 2 + hf)) for hf in range(2)]
                for hf in range(2):
                    bank, bname = banks[hf]
                    for j in range(4):
                        kc = hf * 4 + j
                        tr(bank[:, j * 128:(j + 1) * 128], xT[:, kc, tile * 128:(tile + 1) * 128], ident_f,
                           [("xT", kc, tile // 4), "kf"], [bname])
                    act(junk[hf], bank[:], AF.Square, [bname], [("junk", hf), ("ss", o, hf)], accum_out=ss[:, hf:hf + 1])
                tt("dve", ss[:, 2:3], ss[:, 0:1], ss[:, 1:2], ALU.add, [("ss", o, 0), ("ss", o, 1)], [("ss", o, 2)])
                act(ss[:, 2:3], ss[:, 2:3], AF.Ln, [("ss", o, 2)], [("ss", o, 2)], scale=1.0 / D, bias=EPS)
                act(ss[:, 3:4], ss[:, 2:3], AF.Exp, [("ss", o, 2)], [("ss", o, 3)], scale=-0.5)
                for hf in range(2):
                    bank, bname = banks[hf]
                    stt(ot[o][:, hf * 512:(hf + 1) * 512], bank[:], ss[:, 3:4], gbc[:, hf * 512:(hf + 1) * 512],
                        ALU.mult, ALU.mult, [bname, ("ss", o, 3), "gbc"], [("ot", o)])
                fin.append(dma("sp", out_d[tile * 128:(tile + 1) * 128, :], ot[o], "out%d" % o, [("ot", o)], []))
            return fin

        final_ops = []
        if stage in ("full", "mix", "l0", "hg"):
            lower_bounds()
        nlayers = DEPTH if stage == "full" else 1
        if stage in ("conv", "na", "hg"):
            P.op("pool", lambda e: e.memset(BIG[:, 0:KC * T], 0.0), (), mixall + pages(BIG[:, 0:KC * T]))
        for layer in range(nlayers):
            rmsnorm(C_MIXG + layer * KC, "m%d" % layer)
            if stage == "norm":
                break
            if stage in ("full", "mix", "l0", "conv"):
                conv_mixer(layer)
            if stage in ("full", "mix", "l0", "na"):
                na_mixer(layer)
            if stage in ("full", "mix", "l0", "hg"):
                for h in range(4):
                    hgrn2_head(layer, h)
            if stage in ("mix", "conv", "na", "hg"):
                break
            out_proj(layer)
            rmsnorm(C_FFNG + layer * KC, "f%d" % layer)
            ffn(layer)
        if dbg is not None:
            name = dbg[0]
            if name == "hT":
                src = hT[:].rearrange("p k t -> p (k t)")
                rd = [("hT", kc, b) for kc in range(KC) for b in range(4)]
            elif name == "mixT":
                src = mixT.rearrange("p k t -> p (k t)")
                rd = mixall
            elif name == "xT":
                src = xT[:].rearrange("p k t -> p (k t)")
                rd = [("xT", kc, b) for kc in range(KC) for b in range(4)]
            final_ops.append(dma("sp", dbg_d, src, "dbg", rd, []))
        if stage in ("full", "l0"):
            final_ops += final_out()
        P.emit(final_ops)
    return nc


def _consts():
    kfc = np.zeros((128, NKF), np.float32)
    s = np.arange(128)[:, None]
    c = np.arange(128)[None, :]
    same = (s // HCH) == (c // HCH)
    kfc[:, K_IDF:K_IDF + 128] = np.eye(128, dtype=np.float32)
    _unused_K_UF = (same & (s <= c))
    _unused_K_UB = (same & (s >= c))
    _unused_K_MF = (same & (s > c))
    _unused_K_MB = (same & (s < c))
    kfc[:, K_SCM:K_SCM + 512] = ((np.arange(512) % HCH) != 0)[None, :]
    kbc = np.zeros((128, NKB), np.float32)
    kbc[:, B_ID:B_ID + 128] = np.eye(128, dtype=np.float32)
    kbc[:, B_ONES:B_ONES + 128] = 1.0
    kbc[:, B_OBD:B_OBD + 128] = ((s // 64) == (c // 64))
    kbc[:, B_UF:B_UF + 128] = (same & (s <= c))
    kbc[:, B_UB:B_UB + 128] = (same & (s >= c))
    kbc[:, B_NUF:B_NUF + 128] = -1.0 * (same & (s <= c))
    kbc[:, B_NUB:B_NUB + 128] = -1.0 * (same & (s >= c))
    kcol = np.arange(64)[:, None]
    qcol = np.arange(64)[None, :]
    qs = np.clip(qcol - 8, 0, 48)
    valid = (kcol >= qs) & (kcol < qs + 16)
    colneg = np.where(valid, 0.0, NEGM).astype(np.float32)
    colneg = np.concatenate([colneg, colneg], axis=0)
    return kfc, kbc, colneg


def _layout_small(inp):
    cols = np.zeros((128, NCOLS), np.float32)
    for l in range(DEPTH):
        cols[:, C_MIXG + l * 8:C_MIXG + (l + 1) * 8] = inp["mix_norm_g"][l].reshape(8, 128).T
        cols[:, C_FFNG + l * 8:C_FFNG + (l + 1) * 8] = inp["ffn_norm_g"][l].reshape(8, 128).T
        cols[:, C_HGN + l] = inp["hg_norm_g"][l]
        for c in range(2):
            cols[:, C_CONVW + (l * 2 + c) * 31:C_CONVW + (l * 2 + c + 1) * 31] = inp["conv_w"][l][:, c * 128:(c + 1) * 128].T
            cols[:, C_CONVB + l * 2 + c] = inp["conv_b"][l][c * 128:(c + 1) * 128]
            cols[:, C_LNG + l * 2 + c] = inp["conv_ln_g"][l][c * 128:(c + 1) * 128]
            cols[:, C_LNB + l * 2 + c] = inp["conv_ln_b"][l][c * 128:(c + 1) * 128]
    rpb = inp["na_rpb"]
    kcol = np.arange(64)[:, None]
    qcol = np.arange(64)[None, :]
    ci = kcol - qcol + 15
    ok = (ci >= 0) & (ci <= 30)
    cic = np.clip(ci, 0, 30)
    nab = np.zeros((DEPTH, 2, 64, 2, 15, 64), np.float32)
    for l in range(DEPTH):
        for pr in range(2):
            for a in range(2):
                for t in range(15):
                    g = rpb[l, 2 * pr + a, 14 - t][cic]
                    nab[l, a, :, pr, t, :] = np.where(ok, g, np.float32(0.0))
    nab = nab.reshape(DEPTH, 128, 2 * 15 * 64)
    lbraw = np.ascontiguousarray(inp["hg_lower_bounds"].reshape(DEPTH * 2 * 4, 128).T).astype(np.float32)
    fng = np.ascontiguousarray(inp["final_norm_g"].reshape(1, D)).astype(np.float32)
    return cols, nab, lbraw, fng


_NC_CACHE = {}


def _get_nc(stage="full", dbg=None):
    key = (stage, dbg)
    if key not in _NC_CACHE:
        _NC_CACHE[key] = build_program(stage, dbg)
    return _NC_CACHE[key]


def make_in_maps(inp, ncores):
    kfc, kbc, colneg = _consts()
    cols, nab, lbraw, fng = _layout_small(inp)
    f = lambda a: np.ascontiguousarray(np.asarray(a, dtype=np.float32))
    shared = dict(w_in=f(inp["w_in"]), w_out=f(inp["w_out"]), w_gate_up=f(inp["w_gate_up"]), w_down=f(inp["w_down"]),
                  cols=cols, kf=kfc, kb=kbc, lbraw=lbraw, fng=fng, nab=nab, colneg=colneg)
    x = f(inp["x"])
    return [dict(shared, x=x[i]) for i in range(ncores)]


def kernel(**inputs):
    nc = _get_nc("full", None)
    in_maps = make_in_maps(inputs, 8)
    res = run_bass_kernel_spmd(nc, in_maps, core_ids=list(range(8)))
    return np.stack([np.asarray(r["out"], dtype=np.float32) for r in res.results], axis=0)
```

```python
import numpy as np
import concourse.bass as bass
import concourse.mybir as mybir
from concourse.bass_utils import run_bass_kernel_spmd

F32 = mybir.dt.float32
BF16 = mybir.dt.bfloat16
AF = mybir.ActivationFunctionType
ALU = mybir.AluOpType

T = 2048
D = 1024
KC = 8
DIN = 3840
FFN = 2816
NJ = FFN // 128
DEPTH = 2
EPS = 1e-6
NEGM = -30000.0
HCH = 64
NCH = 128 // HCH

C_MIXG = 0
C_FFNG = 16
C_HGN = 32
C_CONVW = 34
C_CONVB = 158
C_LNG = 162
C_LNB = 166
NCOLS = 170

K_IDF = 0
K_UF = 128
K_UB = 256
K_MF = 384
K_MB = 512
K_SCM = 128
NKF = 640
B_ID = 0
B_ONES = 128
B_OBD = 256
B_ZERO = 384
B_UF = 512
B_UB = 640
B_NUF = 768
B_NUB = 896
NKB = 1024


class Prog:
    ENGS = ("pe", "act", "dve", "pool", "sp")

    def __init__(self, nc):
        self.nc = nc
        self.ops = []
        self.lastw = {}
        self.readers = {}
        self.dma_cnt = {}
        self.final_waits = []
        self.last_on = {}
        self.bar = {}

    def barrier(self):
        snap = dict(self.last_on)
        for e in self.ENGS:
            self.bar[e] = set(snap.values())

    def op(self, eng, fn, reads=(), writes=(), dma_slot=None, nobar=False, cost=0.3):
        idx = len(self.ops)
        deps = set()
        if not nobar and self.bar.get(eng):
            deps |= self.bar[eng]
        for t in reads:
            w = self.lastw.get(t)
            if w is not None:
                deps.add(w)
        for t in writes:
            w = self.lastw.get(t)
            if w is not None:
                deps.add(w)
            for r in self.readers.get(t, ()):
                deps.add(r)
        rec = dict(eng=eng, fn=fn, deps=deps, dma_slot=dma_slot, dma_val=None, rawdeps=set(), cost=cost)
        for t in reads:
            w = self.lastw.get(t)
            if w is not None:
                rec["rawdeps"].add(w)
        if dma_slot is not None:
            self.dma_cnt[dma_slot] = self.dma_cnt.get(dma_slot, 0) + 1
            rec["dma_val"] = 16 * self.dma_cnt[dma_slot]
        self.ops.append(rec)
        if dma_slot is None:
            self.last_on[eng] = idx
        for t in reads:
            self.readers.setdefault(t, []).append(idx)
        for t in writes:
            self.lastw[t] = idx
            self.readers[t] = []
        return idx

    def schedule(self):
        import heapq
        ops = self.ops
        n = len(ops)
        children = [[] for _ in range(n)]
        indeg = [0] * n
        for i, o in enumerate(ops):
            o["alldeps"] = set(o["deps"])
            indeg[i] = len(o["alldeps"])
            for d in o["alldeps"]:
                children[d].append(i)
        finish = [0.0] * n
        ready = [0.0] * n
        blevel = [0.0] * n
        for i in range(n - 1, -1, -1):
            m = 0.0
            for c in children[i]:
                if blevel[c] > m:
                    m = blevel[c]
            blevel[i] = m + ops[i]["cost"] + 0.25
        prio = [(-blevel[i], i) for i in range(n)]
        tfree = {e: 0.0 for e in self.ENGS}
        dma_free = 0.0
        timeheap = {e: [] for e in self.ENGS}
        idxheap = {e: [] for e in self.ENGS}
        order = {e: [] for e in self.ENGS}
        for i, o in enumerate(ops):
            if indeg[i] == 0:
                heapq.heappush(timeheap[o["eng"]], (0.0, i))
        done = 0
        while done < n:
            best = None
            for e in self.ENGS:
                th, ih = timeheap[e], idxheap[e]
                while th and th[0][0] <= tfree[e]:
                    heapq.heappush(ih, prio[heapq.heappop(th)[1]])
                if ih:
                    cand = (tfree[e], ih[0][1], e, True)
                elif th:
                    cand = (th[0][0], th[0][1], e, False)
                else:
                    continue
                if best is None or cand[:2] < best[:2]:
                    best = cand
            start, i, e, fromidx = best
            if fromidx:
                heapq.heappop(idxheap[e])
            else:
                heapq.heappop(timeheap[e])
            o = ops[i]
            if o["dma_slot"] is not None:
                tfree[e] = start + (1.05 if e == "pool" else 0.1)
                t0 = max(start, dma_free)
                dma_free = t0 + o["cost"]
                finish[i] = dma_free + 2.0
            else:
                tfree[e] = start + o["cost"]
                finish[i] = tfree[e]
            order[e].append(i)
            done += 1
            for c in children[i]:
                hop = 0.05 if ops[c]["eng"] == e and o["dma_slot"] is None else 0.2
                r = finish[i] + hop
                if r > ready[c]:
                    ready[c] = r
                indeg[c] -= 1
                if indeg[c] == 0:
                    heapq.heappush(timeheap[ops[c]["eng"]], (ready[c], c))
        self.order = order
        self.est_time = max(finish) if n else 0.0

    def emit(self, final_ops):
        nc = self.nc
        ops = self.ops
        self.schedule()
        needed = [False] * len(ops)
        for i, o in enumerate(ops):
            keep = set()
            for d in o["deps"]:
                od = ops[d]
                if od["dma_slot"] is not None:
                    if o["dma_slot"] == od["dma_slot"] and d not in o["rawdeps"]:
                        continue
                    keep.add(d)
                    continue
                if od["eng"] == o["eng"] and o["dma_slot"] is None:
                    if o["eng"] == "pe":
                        continue
                keep.add(d)
            o["deps"] = keep
            for d in keep:
                needed[d] = True
        for d in final_ops:
            needed[d] = True
        cnt = {e: 0 for e in self.ENGS}
        dcnt = {}
        for e in self.ENGS:
            for i in self.order[e]:
                o = ops[i]
                if o["dma_slot"] is not None:
                    dcnt[o["dma_slot"]] = dcnt.get(o["dma_slot"], 0) + 1
                    o["ev"] = ("dma:" + o["dma_slot"], 16 * dcnt[o["dma_slot"]])
                else:
                    if needed[i]:
                        cnt[e] += 1
                        o["sig"] = True
                    else:
                        o["sig"] = False
                    o["ev"] = (e, cnt[e] if needed[i] else None)
        semnames = list(self.ENGS) + ["dma:" + s for s in self.dma_cnt]
        from contextlib import ExitStack
        with ExitStack() as st:
            sems = {}
            for sn in semnames:
                sems[sn] = st.enter_context(nc.semaphore("s_" + sn.replace(":", "_")))
            block = st.enter_context(nc.Block())
            per_eng = self.order

            def run(eng_name, handle):
                seen = {}
                for i in per_eng[eng_name]:
                    o = ops[i]
                    w = {}
                    for d in o["deps"]:
                        sk, v = ops[d]["ev"]
                        assert v is not None
                        if v > w.get(sk, 0):
                            w[sk] = v
                    for sk, v in w.items():
                        if seen.get(sk, 0) >= v:
                            continue
                        handle.wait_ge(sems[sk], v)
                        seen[sk] = v
                    ins = o["fn"](handle)
                    if o["dma_slot"] is not None:
                        ins.then_inc(sems["dma:" + o["dma_slot"]], 16)
                    elif o["sig"]:
                        ins.then_inc(sems[eng_name], 1)
                if eng_name == "sp":
                    w = {}
                    for d in final_ops:
                        sk, v = ops[d]["ev"]
                        if v > w.get(sk, 0):
                            w[sk] = v
                    for sk, v in w.items():
                        handle.wait_ge(sems[sk], v)

            @block.tensor
            def _(e):
                run("pe", e)

            @block.scalar
            def _(e):
                run("act", e)

            @block.vector
            def _(e):
                run("dve", e)

            @block.gpsimd
            def _(e):
                run("pool", e)

            @block.sync
            def _(e):
                run("sp", e)


def mk_ap(base, extra_off, dims):
    return bass.AP(tensor=base.tensor, offset=base.offset + extra_off,
                   ap=[list(base.ap[0])] + [list(d) for d in dims])


def build_program(stage="full", dbg=None):
    nc = bass.Bass("TRN2", target_bir_lowering=False)
    P = Prog(nc)
    x_d = nc.dram_tensor("x", [T, D], F32, kind="ExternalInput").ap()
    w_in_d = nc.dram_tensor("w_in", [DEPTH, D, DIN], F32, kind="ExternalInput").ap()
    w_out_d = nc.dram_tensor("w_out", [DEPTH, D, D], F32, kind="ExternalInput").ap()
    w_gu_d = nc.dram_tensor("w_gate_up", [DEPTH, D, 2 * FFN], F32, kind="ExternalInput").ap()
    w_dn_d = nc.dram_tensor("w_down", [DEPTH, FFN, D], F32, kind="ExternalInput").ap()
    cols_d = nc.dram_tensor("cols", [128, NCOLS], F32, kind="ExternalInput").ap()
    kf_d = nc.dram_tensor("kf", [128, NKF], F32, kind="ExternalInput").ap()
    kb_d = nc.dram_tensor("kb", [128, NKB], F32, kind="ExternalInput").ap()
    lbraw_d = nc.dram_tensor("lbraw", [128, 16], F32, kind="ExternalInput").ap()
    fng_d = nc.dram_tensor("fng", [1, D], F32, kind="ExternalInput").ap()
    nab_d = nc.dram_tensor("nab", [DEPTH, 128, 2 * 15 * 64], F32, kind="ExternalInput").ap()
    colneg_d = nc.dram_tensor("colneg", [128, 64], F32, kind="ExternalInput").ap()
    out_d = nc.dram_tensor("out", [T, D], F32, kind="ExternalOutput").ap()
    dbg_d = None
    if dbg is not None:
        dbg_d = nc.dram_tensor("dbg", [128, dbg[1]], dbg[2], kind="ExternalOutput").ap()

    from contextlib import ExitStack
    with ExitStack() as st:
        def sb(name, shape, dt):
            return st.enter_context(nc.sbuf_tensor(name, shape, dt))

        def ps(name):
            return st.enter_context(nc.psum_tensor(name, [128, 512], F32))

        xT = sb("xT", [128, KC, T], F32)
        hT = sb("hT", [128, KC, T], BF16)
        BIG = sb("BIG", [128, 41984], BF16)
        WB = [sb("WB%d" % i, [128, 6144], BF16) for i in range(2)]
        cols = sb("cols_sb", [128, NCOLS], F32)
        kf = sb("kf_sb", [128, NKF], F32)
        kb = sb("kb_sb", [128, NKB], BF16)
        lbc = sb("lbc", [128, 32], F32)
        PS = [ps("ps%d" % i) for i in range(8)]

        mixT = BIG[:, 0:KC * T].rearrange("p (k t) -> p k t", k=KC)
        SCR = BIG[:, KC * T:41984]
        SCRF = SCR.bitcast(F32)

        def colap(c):
            return cols[:, c:c + 1]

        def fsz(ap):
            n = 1
            for d in ap.shape[1:]:
                n *= int(d)
            return n

        PAGE = 256

        def pages(*aps):
            out = []
            for ap in aps:
                if ap is None or isinstance(ap, (int, float)):
                    continue
                if ap.tensor.name != "BIG":
                    continue
                esz = mybir.dt.size(ap.dtype)
                dims = [list(d) for d in ap.ap]
                pstep = int(dims[0][0])
                lo = (int(ap.offset) % pstep) * esz
                ext = 0
                for st, cnt in dims[1:]:
                    ext += (int(cnt) - 1) * abs(int(st))
                hi = lo + (ext + 1) * esz
                for k in range(lo // PAGE, (hi - 1) // PAGE + 1):
                    out.append(("pg", k))
            return out

        def ecost(eng, n):
            if eng == "act":
                return 0.2 + n * 0.00084
            if eng == "dve":
                return 0.16 + n * 0.00104
            return 0.3 + n * 0.002

        def dma(q, out, in_, slot, reads=(), writes=(), nobar=False):
            nbytes = fsz(out) * int(out.shape[0]) * 4
            return P.op(q, lambda e: e.dma_start(out=out, in_=in_), list(reads) + pages(in_), list(writes) + pages(out),
                        dma_slot=slot, nobar=nobar, cost=nbytes / 330e3)

        def mm(out, lhsT, rhs, start, stop, reads, writes, **kw):
            return P.op("pe", lambda e: e.matmul(out, lhsT, rhs, start=start, stop=stop, **kw),
                        list(reads) + pages(lhsT, rhs), writes, cost=0.03 + fsz(rhs) / 2400.0)

        def tr(out, in_, ident, reads, writes):
            return P.op("pe", lambda e: e.transpose(out, in_, ident), list(reads) + pages(in_), writes, cost=0.12)

        def act(out, in_, func, reads, writes, scale=1.0, bias=0.0, accum_out=None):
            kw = {}
            if accum_out is not None:
                kw["accum_out"] = accum_out
            return P.op("act", lambda e: e.activation(out=out, in_=in_, func=func, scale=scale, bias=bias, **kw),
                        list(reads) + pages(in_, bias if not isinstance(bias, float) else None),
                        list(writes) + pages(out, accum_out), cost=ecost("act", fsz(out)))

        def tt(eng, out, in0, in1, op, reads, writes):
            return P.op(eng, lambda e: e.tensor_tensor(out=out, in0=in0, in1=in1, op=op),
                        list(reads) + pages(in0, in1), list(writes) + pages(out), cost=ecost(eng, fsz(out)))

        def ts(eng, out, in0, s1, op0, reads, writes, s2=None, op1=None):
            if op1 is None:
                return P.op(eng, lambda e: e.tensor_scalar(out=out, in0=in0, scalar1=s1, scalar2=None, op0=op0),
                            list(reads) + pages(in0, s1), list(writes) + pages(out), cost=ecost(eng, fsz(out)))
            return P.op(eng, lambda e: e.tensor_scalar(out=out, in0=in0, scalar1=s1, scalar2=s2, op0=op0, op1=op1),
                        list(reads) + pages(in0, s1, s2), list(writes) + pages(out), cost=ecost(eng, fsz(out)))

        def stt(out, in0, scalar, in1, op0, op1, reads, writes):
            return P.op("dve", lambda e: e.scalar_tensor_tensor(out=out, in0=in0, scalar=scalar, in1=in1,
                                                                 op0=op0, op1=op1),
                        list(reads) + pages(in0, scalar, in1), list(writes) + pages(out),
                        cost=ecost("dve", fsz(out)) + 0.06)

        def cp(eng, out, in_, reads, writes):
            if eng == "act":
                return P.op("act", lambda e: e.copy(out=out, in_=in_), list(reads) + pages(in_),
                            list(writes) + pages(out), cost=ecost("act", fsz(out)))
            return P.op(eng, lambda e: e.tensor_copy(out=out, in_=in_), list(reads) + pages(in_),
                        list(writes) + pages(out), cost=ecost(eng, fsz(out)))

        wslot_ctr = [0]

        def wslot():
            i = wslot_ctr[0] % 2
            wslot_ctr[0] += 1
            return i

        dma("sp", cols[:], cols_d, "c_cols", (), ["cols"])
        dma("sp", kf[:], kf_d, "c_kf", (), ["kf"])
        kbs = SCRF[:, 6656:6656 + NKB]
        dma("sp", kbs, kb_d, "c_kb", (), ["kbs"])
        cp("dve", kb[:], kbs, ["kbs"], ["kb"])
        ident_f = kf[:, K_IDF:K_IDF + 128]
        ident_b = kb[:, B_ID:B_ID + 128]
        ones_b = kb[:, B_ONES:B_ONES + 128]

        xin = SCRF[:, 0:4096].rearrange("p (a d) -> p a d", a=4)
        for tile in range(16):
            s = tile % 4
            dma("sp", xin[:, s, :], x_d[tile * 128:(tile + 1) * 128, :], "xin%d" % s, (), [("xin", s)])
            for half in range(2):
                bank = PS[(tile * 2 + half) % 4]
                bname = "ps%d" % ((tile * 2 + half) % 4)
                for j in range(4):
                    kc = half * 4 + j
                    tr(bank[:, j * 128:(j + 1) * 128], xin[:, s, kc * 128:(kc + 1) * 128], ident_f,
                       [("xin", s), "kf"], [bname])
                eng = "dve" if half == 0 else "act"
                cp(eng, xT[:, half * 4:half * 4 + 4, tile * 128:(tile + 1) * 128],
                   bank[:].rearrange("p (j c) -> p j c", j=4), [bname],
                   [("xT", kc, tile // 4) for kc in range(half * 4, half * 4 + 4)])

        def rmsnorm(gcol0, tag):
            sq = SCR[:, 8192:12288].rearrange("p (k t) -> p k t", k=KC)
            rstd = SCRF[:, 6144:6656]
            for blk in range(4):
                tsl = slice(blk * 512, (blk + 1) * 512)
                for kc in range(KC):
                    act(sq[:, kc, :], xT[:, kc, tsl], AF.Square, [("xT", kc, blk)], [("sq", kc)])
                for kc in range(KC):
                    mm(PS[4][:], ones_b, sq[:, kc, :], kc == 0, kc == KC - 1, [("sq", kc), "kb"], ["ps4"])
                act(rstd, PS[4][:], AF.Ln, ["ps4"], ["rstd"], scale=1.0 / D, bias=EPS)
                act(rstd, rstd, AF.Exp, ["rstd"], ["rstd"], scale=-0.5)
                for kc in range(KC):
                    stt(hT[:, kc, tsl], xT[:, kc, tsl], colap(gcol0 + kc), rstd, ALU.mult, ALU.mult,
                        [("xT", kc, blk), "rstd", "cols"], [("hT", kc, blk)])

        def load_w_in(layer, groups, after=()):
            si = wslot()
            slot = WB[si]
            aps = []
            off = 0
            wv = w_in_d[layer].rearrange("(kc p) n -> p kc n", p=128)
            for (c0, n) in groups:
                ap = slot[:, off:off + KC * n].rearrange("p (k n) -> p k n", k=KC)
                dma("pool", ap, wv[:, :, c0:c0 + n], "wb%d" % si, list(after), [("WB", si)], nobar=True)
                aps.append(ap)
                off += KC * n
            return si, aps

        def proj_fm(bank, bname, w_ap, si, blk, c0=0, n=128):
            for kc in range(KC):
                mm(bank[0:n, :], w_ap[:, kc, c0:c0 + n], hT[:, kc, blk * 512:(blk + 1) * 512], kc == 0, kc == KC - 1,
                   [("WB", si), ("hT", kc, blk)], [bname])

        def lower_bounds():
            dma("sp", lbc[:, 0:16], lbraw_d, "c_lb", (), ["lbc"])
            act(lbc[:, 0:16], lbc[:, 0:16], AF.Exp, ["lbc"], ["lbc"])
            tt("dve", lbc[:, 24:32], lbc[:, 0:8], lbc[:, 8:16], ALU.add, ["lbc"], ["lbc"])
            P.op("dve", lambda e: e.reciprocal(out=lbc[:, 24:32], in_=lbc[:, 24:32]), ["lbc"], ["lbc"])
            tt("dve", lbc[:, 16:24], lbc[:, 8:16], lbc[:, 24:32], ALU.mult, ["lbc"], ["lbc"])
            ts("dve", lbc[:, 24:32], lbc[:, 16:24], -1.0, ALU.mult, ["lbc"], ["lbc"], s2=1.0, op1=ALU.add)

        def conv_mixer(layer):
            CH = 2
            ub = SCR[:, 0:CH * 2080].rearrange("p (c t) -> p c t", c=CH)
            dg = SCR[:, 4160:4160 + 62 * 128].rearrange("p (c j k) -> p c j k", c=CH, j=31)
            FB = 6080
            a2 = SCRF[:, FB:FB + 1024].rearrange("p (c t) -> p c t", c=CH)
            tmp = SCRF[:, FB + 1024:FB + 1536]
            tmp2 = SCRF[:, FB + 1536:FB + 2048]
            sg = SCRF[:, FB + 2048:FB + 2560]
            sqb = SCR[:, 2 * (FB + 2560):2 * (FB + 2560) + 1024].rearrange("p (c t) -> p c t", c=CH)
            accb = SCR[:, 2 * (FB + 2560) + 1024:2 * (FB + 2560) + 2048].rearrange("p (c t) -> p c t", c=CH)
            si, (wa, wg) = load_w_in(layer, [(3328, 256), (3584, 256)],
                                     after=[("xin", i) for i in range(4)] if layer == 0 else ())
            U = lambda c: [("u", c, b) for b in (-1, 0, 1, 2, 3, 4)]
            for c in range(CH):
                P.op("pool", lambda e, c=c: e.memset(ub[:, c, 0:16], 0.0), (), [("u", c, -1)] + pages(ub[:, c, 0:16]))
                P.op("pool", lambda e, c=c: e.memset(ub[:, c, 16 + T:2080], 0.0), (),
                     [("u", c, 4)] + pages(ub[:, c, 16 + T:2080]))
                wc0 = C_CONVW + (layer * 2 + c) * 31
                tt("dve", dg[:, c, :, :], mk_ap(ident_b, 0, [[0, 31], [1, 128]]),
                   mk_ap(cols[:, wc0:wc0 + 31], 0, [[1, 31], [0, 128]]), ALU.mult, ["kb", "cols"], [("dg", c)])
            for c in range(CH):
                for blk in range(4):
                    proj_fm(PS[0], "ps0", wa, si, blk, c * 128)
                    proj_fm(PS[1], "ps1", wg, si, blk, c * 128)
                    act(tmp, PS[1][:], AF.Exp, ["ps1"], ["ctmp"], scale=-1.0)
                    act(tmp, tmp, AF.Ln, ["ctmp"], ["ctmp"], bias=1.0)
                    act(tmp, tmp, AF.Exp, ["ctmp"], ["ctmp"], scale=-1.0)
                    tt("dve", ub[:, c, 16 + blk * 512:16 + (blk + 1) * 512], PS[0][:], tmp, ALU.mult,
                       ["ps0", "ctmp"], [("u", c, blk)])
            for blk in range(4):
                tsl = slice(blk * 512, (blk + 1) * 512)
                for c in range(CH):
                    bank, bname = PS[2 + c], "ps%d" % (2 + c)
                    for j in range(31):
                        mm(bank[:], dg[:, c, j, :], ub[:, c, blk * 512 + j + 1:blk * 512 + j + 1 + 512], j == 0, j == 30,
                           [("dg", c)] + [("u", c, b) for b in (blk - 1, blk, blk + 1)], [bname])
                    act(a2[:, c, :], bank[:], AF.Identity, [bname, "cols"], [("a2", c)],
                        bias=colap(C_CONVB + layer * 2 + c))
                for c in range(CH):
                    act(sqb[:, c, :], a2[:, c, :], AF.Square, [("a2", c)], [("csq", c)])
                    cp("dve", accb[:, c, :], a2[:, c, :], [("a2", c)], [("cab", c)])
                for c in range(CH):
                    mm(PS[4][:], ones_b, accb[:, c, :], c == 0, c == CH - 1, [("cab", c), "kb"], ["ps4"])
                for c in range(CH):
                    mm(PS[5][:], ones_b, sqb[:, c, :], c == 0, c == CH - 1, [("csq", c), "kb"], ["ps5"])
                ts("dve", tmp, PS[4][:], 1.0 / 256, ALU.mult, ["ps4"], ["ctmp"])
                tt("dve", tmp2, tmp, tmp, ALU.mult, ["ctmp"], ["ctmp2"])
                stt(tmp2, PS[5][:], 1.0 / 256, tmp2, ALU.mult, ALU.subtract, ["ps5", "ctmp2"], ["ctmp2"])
                act(tmp2, tmp2, AF.Ln, ["ctmp2"], ["ctmp2"], bias=EPS)
                act(tmp2, tmp2, AF.Exp, ["ctmp2"], ["ctmp2"], scale=-0.5)
                for c in range(CH):
                    a = a2[:, c, :]
                    A = ("a2", c)
                    tt("dve", a, a, tmp, ALU.subtract, [A, "ctmp"], [A])
                    tt("dve", a, a, tmp2, ALU.mult, [A, "ctmp2"], [A])
                    ts("dve", a, a, colap(C_LNG + layer * 2 + c), ALU.mult, [A, "cols"], [A],
                       s2=colap(C_LNB + layer * 2 + c), op1=ALU.add)
                    act(sg, a, AF.Exp, [A], ["csg"], scale=-1.0)
                    act(sg, sg, AF.Ln, ["csg"], ["csg"], bias=1.0)
                    act(sg, sg, AF.Exp, ["csg"], ["csg"], scale=-1.0)
                    tt("dve", mixT[:, 6 + c, tsl], a, sg, ALU.mult, [A, "csg"], [("mixT", 6 + c, blk)])

        def kr0(r):
            return min(max(r - 4, 0), 24)

        def na_mixer(layer):
            si, (wq, wk, wv) = load_w_in(layer, [(2560, 256), (2816, 256), (3072, 256)],
                                         after=[("xin", i) for i in range(4)] if layer == 0 else ())
            qT = SCR[:, 0:2048]
            Kbd = SCR[:, 2048:6144].rearrange("p (r k) -> p r k", r=32)
            Vbd = SCR[:, 6144:10240].rearrange("p (r k) -> p r k", r=32)
            tblf = SCRF[:, 5120:7040].rearrange("p (a t q) -> p a t q", a=2, t=15)
            tblf3 = SCRF[:, 5120:7040].rearrange("p (a q) -> p a q", q=64)
            PT = [SCR[:, 14080 + i * 512:14080 + (i + 1) * 512] for i in range(2)] + [SCR[:, 22400:22912]]
            rec = SCRF[:, 7552:8064]
            TMP = [SCRF[:, 8064 + i * 512:8064 + (i + 1) * 512] for i in range(2)] + [SCRF[:, 11520:12032]]
            cng = SCRF[:, 9088:9152]
            VTbd_flat = SCR[:, 18304:22400]
            VTbd = VTbd_flat.rearrange("p (r k) -> p r k", r=32)
            dma("sp", SCRF[:, 5120:7040], nab_d[layer], "c_nab", (), ["tblf"])
            dma("sp", cng, colneg_d, "c_cng", (), ["cng"])
            tt("dve", tblf3, tblf3, mk_ap(cng, 0, [[0, 30], [1, 64]]), ALU.add, ["tblf", "cng"], ["tblf"])
            obd = kb[:, B_OBD:B_OBD + 128]
            zer = kb[:, B_ZERO:B_ZERO + 128]
            P.op("pool", lambda e: e.memset(SCR[:, 2048:6144], 0.0), (), ["Kbd"] + pages(SCR[:, 2048:6144]), cost=3.5)
            P.op("pool", lambda e: e.memset(VTbd_flat, 0.0), (), ["VTbd"] + pages(VTbd_flat), cost=3.5)
            for pr in range(2):
                for blk in range(4):
                    tsl = slice(blk * 512, (blk + 1) * 512)
                    proj_fm(PS[0], "ps0", wq, si, blk, pr * 128)
                    cp("act", qT[:, tsl], PS[0][:], ["ps0"], [("qT", blk)])
                    proj_fm(PS[1], "ps1", wk, si, blk, pr * 128)
                    for a_ in range(2):
                        cp("dve" if a_ == 0 else "act",
                           Kbd[a_ * 64:(a_ + 1) * 64, blk * 8:(blk + 1) * 8, a_ * 64:(a_ + 1) * 64],
                           PS[1][a_ * 64:(a_ + 1) * 64, :].rearrange("p (r k) -> p r k", r=8), ["ps1"], ["Kbd"])
                for blk in range(4):
                    proj_fm(PS[2], "ps2", wv, si, blk, pr * 128)
                    for a_ in range(2):
                        cp("dve" if a_ == 0 else "act",
                           VTbd[a_ * 64:(a_ + 1) * 64, blk * 8:(blk + 1) * 8, a_ * 64:(a_ + 1) * 64],
                           PS[2][a_ * 64:(a_ + 1) * 64, :].rearrange("p (r k) -> p r k", r=8), ["ps2"], ["VTbd"])
                for kr8 in range(4):
                    bank, bname = PS[3], "ps3"
                    bankb = bank[:].bitcast(BF16)
                    for i8 in range(8):
                        kr = kr8 * 8 + i8
                        tr(bankb[:, i8 * 128:(i8 + 1) * 128], VTbd[:, kr, :], ident_b, ["VTbd", "kb"], [bname])
                    cp("dve" if kr8 % 2 == 0 else "act", Vbd[:, kr8 * 8:(kr8 + 1) * 8, :],
                       bankb.rearrange("p (r k) -> p r k", r=8), [bname], ["Vbd"])
                unit = 0
                for g in range(4):
                    r0 = 8 * g
                    Ob, Obn = PS[4 + (g % 2) * 2], "ps%d" % (4 + (g % 2) * 2)
                    Db, Dbn = PS[5 + (g % 2) * 2], "ps%d" % (5 + (g % 2) * 2)
                    mm(Ob[:], zer, qT[:, 0:512], True, False, ["kb", ("qT", 0)], [Obn])
                    mm(Db[:], zer, qT[:, 0:512], True, False, ["kb", ("qT", 0)], [Dbn])
                    krs = list(range(kr0(r0), kr0(r0 + 7) + 8))
                    for ki, kr in enumerate(krs):
                        rows = [r for r in range(r0, r0 + 8) if kr0(r) <= kr <= kr0(r) + 7]
                        ra, rb = rows[0], rows[-1]
                        n = (rb - ra + 1) * 64
                        t0 = ra - kr + 7
                        c0 = (ra - r0) * 64
                        sbi = unit % 3
                        unit += 1
                        Sb_, Sbn = PS[sbi], "ps%d" % sbi
                        pt, tmp = PT[sbi], TMP[sbi]
                        qdeps = sorted(set([("qT", (ra * 64) // 512), ("qT", (rb * 64 + 63) // 512)]))
                        mm(Sb_[:, 0:n], Kbd[:, kr, :], qT[:, ra * 64:ra * 64 + n], True, True, ["Kbd"] + qdeps, [Sbn])
                        stt(tmp[:, 0:n], Sb_[:, 0:n], 0.125,
                            tblf[:, pr, t0:t0 + (rb - ra + 1), :].rearrange("p t q -> p (t q)"),
                            ALU.mult, ALU.add, [Sbn, "tblf"], [("natmp", sbi)])
                        act(pt[:, 0:n], tmp[:, 0:n], AF.Exp, [("natmp", sbi)], [("PT", sbi)])
                        last = ki == len(krs) - 1
                        mm(Ob[:, c0:c0 + n], Vbd[:, kr, :], pt[:, 0:n], False, last, ["Vbd", ("PT", sbi)], [Obn])
                        mm(Db[:, c0:c0 + n], obd, pt[:, 0:n], False, last, ["kb", ("PT", sbi)], [Dbn])
                    act(rec, Db[:], AF.Ln, [Dbn], ["narec"])
                    act(rec, rec, AF.Exp, ["narec"], ["narec"], scale=-1.0)
                    tt("dve", mixT[:, 4 + pr, r0 * 64:r0 * 64 + 512], Ob[:], rec, ALU.mult, [Obn, "narec"],
                       [("mixT", 4 + pr, g)])

        def hgrn2_head(layer, h):
            si, (wq, wff, wfb, wi, wg) = load_w_in(layer, [(h * 128, 128), (512 + h * 128, 128), (1024 + h * 128, 128),
                                                           (1536 + h * 128, 128), (2048 + h * 128, 128)])
            W = ("WB", si)
            sqT = SCR[:, 0:2048]
            vtm = SCR[:, 2048:4096].rearrange("p (t d) -> p t d", t=16)
            oacc = SCRF[:, 2048:4096]
            AE = []
            for i in range(2):
                b0 = 4096 + i * 2560
                AE.append([SCRF[:, b0 + j * 512:b0 + (j + 1) * 512] for j in range(5)])
            kin = SCR[:, 2 * 9216:2 * 9216 + 512]
            kkT = SCR[:, 2 * 9472:2 * 9472 + 512]
            HO = []
            for i in range(2):
                fb = 9728 + i * 1088
                HO.append(dict(qin=SCR[:, 2 * fb:2 * fb + 512], PTm=SCR[:, 2 * fb + 512:2 * fb + 1024],
                               kkx=SCR[:, 2 * fb + 1024:2 * fb + 1024 + 512 * NCH].rearrange("p (t c d) -> p t c d", t=4, c=NCH),
                               kkx_flat=SCR[:, 2 * fb + 1024:2 * fb + 1024 + 512 * NCH],
                               dec=SCRF[:, fb + 1024:fb + 1040], i=i))
            X = [SCRF[:, 11904 + i * 256:11904 + (i + 1) * 256] for i in range(2)]
            Sbb = [SCR[:, 2 * 12416 + i * 256:2 * 12416 + (i + 1) * 256] for i in range(2)]
            zer = kb[:, B_ZERO:B_ZERO + 128]
            scm = kf[:, K_SCM:K_SCM + 512]
            hgn = colap(C_HGN + layer)
            for blk in range(4):
                tsl = slice(blk * 512, (blk + 1) * 512)
                proj_fm(PS[6], "ps6", wq, si, blk)
                A_ = AE[blk % 2][0]
                An = "A%d" % (blk % 2)
                act(A_, PS[6][:], AF.Exp, ["ps6"], [An], scale=-1.0)
                act(A_, A_, AF.Ln, [An], [An], bias=1.0)
                act(A_, A_, AF.Exp, [An], [An], scale=-1.0)
                tt("dve", sqT[:, tsl], PS[6][:], A_, ALU.mult, ["ps6", An], [("sqT", blk)])
                for t4 in range(4):
                    tile = blk * 4 + t4
                    for kc in range(KC):
                        mm(PS[7][:, t4 * 128:(t4 + 1) * 128], hT[:, kc, tile * 128:(tile + 1) * 128], wi[:, kc, :],
                           kc == 0, kc == KC - 1, [W, ("hT", kc, blk)], ["ps7"])
                cp("dve", vtm[:, blk * 4:(blk + 1) * 4, :], PS[7][:].rearrange("p (t d) -> p t d", t=4), ["ps7"],
                   [("vtm", blk)])
            for ho in HO:
                P.op("pool", lambda e, ho=ho: e.memset(ho["kkx_flat"], 0.0), (), [("kkx", ho["i"])] + pages(ho["kkx_flat"]),
                     cost=1.0)

            touched = set()
            state = dict(xi=0, prevSb=None)

            def front(u):
                dirn, blk, ho = u["dirn"], u["blk"], HO[u["hi"]]
                hi = u["hi"]
                tsl = slice(blk * 512, (blk + 1) * 512)
                wf_ = wfb if dirn else wff
                A_, B_, C_, D_, E_ = AE[u["ae"]]
                nA, nB, nC, nD, nE = ["%s%d" % (x, u["ae"]) for x in "ABCDE"]
                proj_fm(PS[0], "ps0", wf_, si, blk)
                act(B_, PS[0][:], AF.Exp, ["ps0"], [nB], scale=-1.0)
                act(B_, B_, AF.Ln, [nB], [nB], bias=1.0)
                act(A_, B_, AF.Exp, [nB], [nA], scale=-1.0)
                if layer == 0:
                    sg = -1.0
                else:
                    ci = dirn * 4 + h
                    ts("dve", A_, A_, lbc[:, 24 + ci:25 + ci], ALU.mult, [nA, "lbc"], [nA],
                       s2=lbc[:, 16 + ci:17 + ci], op1=ALU.add)
                    act(B_, A_, AF.Ln, [nA], [nB])
                    sg = 1.0
                P.op("dve", lambda e: e.tensor_tensor_scan(out=C_, data0=scm, data1=B_, initial=0.0,
                                                          op0=ALU.mult, op1=ALU.add), [nB, "kf"] + pages(B_), [nC] + pages(C_), cost=1.25)
                blast = mk_ap(C_, HCH - 1, [[HCH, 512 // HCH], [0, HCH]])
                c3 = C_.rearrange("p (n c) -> p n c", c=HCH)
                d3 = D_.rearrange("p (n c) -> p n c", c=HCH)
                act(ho["dec"][:, 0:512 // HCH], mk_ap(C_, HCH - 1, [[HCH, 512 // HCH]]), AF.Exp, [nC], [("dec", hi)],
                    scale=sg)
                if dirn == 0:
                    act(E_, C_, AF.Exp, [nC], [nE], scale=-sg)
                    act(C_, C_, AF.Exp, [nC], [nC], scale=sg)
                    Eb, Enb = C_, E_
                    ebn, enbn = nC, nE
                else:
                    tt("dve", d3, blast, c3, ALU.subtract, [nC], [nD])
                    tt("dve", E_, B_, D_, ALU.add, [nB, nD], [nE])
                    act(C_, E_, AF.Exp, [nE], [nC], scale=-sg)
                    act(E_, E_, AF.Exp, [nE], [nE], scale=sg)
                    Eb, Enb = E_, C_
                    ebn, enbn = nE, nC
                tt("dve", ho["qin"], sqT[:, tsl], Eb, ALU.mult, [("sqT", blk), ebn], [("qin", hi)])
                stt(D_, A_, 1.0, Enb, ALU.subtract, ALU.mult, [nA, enbn], [nD])
                cp("dve", kin, D_, [nD], ["kin"])
                tt("dve", kkT.rearrange("p (n c) -> p n c", c=HCH), d3,
                   mk_ap(ho["dec"], 0, [[1, 512 // HCH], [0, HCH]]), ALU.mult, [nD, ("dec", hi)], ["kkT"])
                kb_ = PS[1][:].bitcast(BF16)
                for t4 in range(4):
                    cs = slice(t4 * 128, (t4 + 1) * 128)
                    tr(kb_[:, cs], kkT[:, cs], ident_b, ["kkT", "kb"], ["ps1"])
                for ch in range(NCH):
                    psl = slice(ch * HCH, (ch + 1) * HCH)
                    cp("act", ho["kkx"][psl, :, ch, :],
                       kb_[psl, 0:512].rearrange("p (t d) -> p t d", t=4), ["ps1"], [("kkx", hi)])
                for t4 in range(4):
                    cs = slice(t4 * 128, (t4 + 1) * 128)
                    mm(PS[2][:, cs], kin[:, cs], ho["qin"][:, cs], True, True, ["kin", ("qin", hi)], ["ps2"])
                Umask = kb[:, (B_NUB if dirn else B_NUF):(B_NUB if dirn else B_NUF) + 128]
                tt("dve", ho["PTm"].rearrange("p (t c) -> p t c", t=4), PS[2][:].rearrange("p (t c) -> p t c", t=4),
                   mk_ap(Umask, 0, [[0, 4], [1, 128]]), ALU.mult, ["ps2", "kb"], [("PTm", hi)])

            def back(u):
                dirn, blk, ho = u["dirn"], u["blk"], HO[u["hi"]]
                hi = u["hi"]
                A_, B_, C_, D_, E_ = AE[u["ae"]]
                nA, nB, nC, nD, nE = ["%s%d" % (x, u["ae"]) for x in "ABCDE"]
                tsl = slice(blk * 512, (blk + 1) * 512)
                order = [3, 2, 1, 0] if dirn else [0, 1, 2, 3]
                corder = list(range(NCH - 1, -1, -1)) if dirn else list(range(NCH))
                if u["first"]:
                    state["prevSb"] = None
                for t4 in order:
                    tile = blk * 4 + t4
                    cs = slice(t4 * 128, (t4 + 1) * 128)
                    dsb, dsn = PS[3 + state["xi"] % 2], "ps%d" % (3 + state["xi"] % 2)
                    xi = state["xi"] % 2
                    Xc, Xp = X[xi], X[1 - xi]
                    for ch in range(NCH):
                        mm(dsb[:, ch * 128:(ch + 1) * 128], ho["kkx"][:, t4, ch, :], vtm[:, tile, :], True, True,
                           [("kkx", hi), ("vtm", blk)], [dsn])
                    for n_, ch in enumerate(corder):
                        c0 = t4 * NCH + ch
                        dcol = ho["dec"][:, c0:c0 + 1]
                        xo = Xc[:, ch * 128:(ch + 1) * 128]
                        if n_ == 0:
                            if state["prevSb"] is None:
                                ts("dve", xo, dsb[:, ch * 128:(ch + 1) * 128], -1.0, ALU.mult, [dsn], [("X", xi)])
                            else:
                                lastch = corder[-1]
                                stt(xo, Xp[:, lastch * 128:(lastch + 1) * 128], dcol, dsb[:, ch * 128:(ch + 1) * 128],
                                    ALU.mult, ALU.subtract, [("X", 1 - xi), ("dec", hi), dsn], [("X", xi)])
                        else:
                            pch = corder[n_ - 1]
                            stt(xo, Xc[:, pch * 128:(pch + 1) * 128], dcol, dsb[:, ch * 128:(ch + 1) * 128],
                                ALU.mult, ALU.subtract, [("X", xi), ("dec", hi), dsn], [("X", xi)])
                    cp("act", Sbb[xi][:, 0:NCH * 128], Xc[:, 0:NCH * 128], [("X", xi)], [("Sb", xi)])
                    mm(PS[5][:, cs], vtm[:, tile, :], ho["PTm"][:, cs], True, False, [("vtm", blk), ("PTm", hi)], ["ps5"])
                    for n_, ch in enumerate(corder):
                        c0 = t4 * 128 + ch * HCH
                        if n_ == 0:
                            if state["prevSb"] is None:
                                lhs, ldep = zer, "kb"
                            else:
                                lastch = corder[-1]
                                lhs, ldep = Sbb[1 - xi][:, lastch * 128:(lastch + 1) * 128], ("Sb", 1 - xi)
                        else:
                            pch = corder[n_ - 1]
                            lhs, ldep = Sbb[xi][:, pch * 128:(pch + 1) * 128], ("Sb", xi)
                        mm(PS[5][:, c0:c0 + HCH], lhs, ho["qin"][:, c0:c0 + HCH], False, n_ == NCH - 1, [ldep, ("qin", hi)], ["ps5"])
                    state["prevSb"] = xi
                    state["xi"] += 1
                if blk not in touched:
                    touched.add(blk)
                    cp("act", oacc[:, tsl], PS[5][:], ["ps5"], [("oacc", blk)])
                else:
                    tt("dve", oacc[:, tsl], oacc[:, tsl], PS[5][:], ALU.add, ["ps5", ("oacc", blk)], [("oacc", blk)])
                    osq = C_.bitcast(BF16)[:, 0:512]
                    act(osq, oacc[:, tsl], AF.Square, [("oacc", blk)], [nC])
                    mm(PS[6][:], ones_b, osq, True, True, [nC, "kb"], ["ps6"])
                    act(B_, PS[6][:], AF.Ln, ["ps6"], [nB], scale=1.0 / 128, bias=EPS)
                    act(B_, B_, AF.Exp, [nB], [nB], scale=-0.5)
                    proj_fm(PS[7], "ps7", wg, si, blk)
                    act(A_, PS[7][:], AF.Exp, ["ps7"], [nA], scale=-1.0)
                    act(A_, A_, AF.Ln, [nA], [nA], bias=1.0)
                    act(A_, A_, AF.Exp, [nA], [nA], scale=-1.0)
                    tt("dve", A_, A_, PS[7][:], ALU.mult, [nA, "ps7"], [nA])
                    tt("dve", oacc[:, tsl], oacc[:, tsl], B_, ALU.mult, [("oacc", blk), nB], [("oacc", blk)])
                    stt(mixT[:, h, tsl], oacc[:, tsl], hgn, A_, ALU.mult, ALU.mult, [("oacc", blk), "cols", nA],
                        [("mixT", h, blk)])

            units = []
            for dirn in (1, 0):
                for n_, blk in enumerate([3, 2, 1, 0] if dirn else [0, 1, 2, 3]):
                    units.append(dict(dirn=dirn, blk=blk, first=(n_ == 0), hi=len(units) % 2, ae=len(units) % 2))
            front(units[0])
            for i, u in enumerate(units):
                if i + 1 < len(units):
                    front(units[i + 1])
                back(u)

        def out_proj(layer):
            wv = w_out_d[layer].rearrange("(kc p) n -> p kc n", p=128)
            waps = []
            for halfn in range(2):
                si = wslot()
                wap = WB[si][:, 0:KC * 512].rearrange("p (k n) -> p k n", k=KC)
                dma("pool", wap, wv[:, :, halfn * 512:(halfn + 1) * 512], "wb%d" % si, (), [("WB", si)], nobar=True)
                waps.append((si, wap))
            cnt = 0
            for blk in range(4):
                tsl = slice(blk * 512, (blk + 1) * 512)
                for fc in range(KC):
                    si, wap = waps[fc // 4]
                    fc4 = fc % 4
                    bi = cnt % 4
                    cnt += 1
                    bank, bname = PS[bi], "ps%d" % bi
                    for kc in range(KC):
                        mm(bank[:], wap[:, kc, fc4 * 128:(fc4 + 1) * 128], mixT[:, kc, tsl], kc == 0, kc == KC - 1,
                           [("WB", si), ("mixT", kc, blk)], [bname])
                    tt("dve", xT[:, fc, tsl], xT[:, fc, tsl], bank[:], ALU.add, [bname, ("xT", fc, blk)],
                       [("xT", fc, blk)])

        def ffn(layer):
            actT = BIG[:, 0:NJ * 1024].rearrange("p (j t) -> p j t", j=NJ)
            gv = w_gu_d[layer].rearrange("(kc p) n -> p kc n", p=128)
            dv = w_dn_d[layer].rearrange("(j p) n -> p j n", p=128)
            SIL = [SCRF[:, 9216 + i * 512:9216 + (i + 1) * 512] for i in range(2)]
            AT = [("BIGall",)]
            cnt = 0
            for half in range(2):
                for sl in range(NJ // 2):
                    si = wslot()
                    gap = WB[si][:, 0:2048].rearrange("p (k n) -> p k n", k=KC)
                    uap = WB[si][:, 2048:4096].rearrange("p (k n) -> p k n", k=KC)
                    dma("pool", gap, gv[:, :, sl * 256:(sl + 1) * 256], "wb%d" % si, (), [("WB", si)], nobar=True)
                    dma("pool", uap, gv[:, :, FFN + sl * 256:FFN + (sl + 1) * 256], "wb%d" % si, (), [("WB", si)], nobar=True)
                    for jj in range(2):
                        j = sl * 2 + jj
                        for b2 in range(2):
                            blk = half * 2 + b2
                            tsl = slice(blk * 512, (blk + 1) * 512)
                            bi = cnt % 2
                            cnt += 1
                            bg, bgn = PS[bi * 2], "ps%d" % (bi * 2)
                            bu, bun = PS[bi * 2 + 1], "ps%d" % (bi * 2 + 1)
                            for kc in range(KC):
                                mm(bg[:], gap[:, kc, jj * 128:(jj + 1) * 128], hT[:, kc, tsl], kc == 0, kc == KC - 1,
                                   [("WB", si), ("hT", kc, blk)], [bgn])
                            for kc in range(KC):
                                mm(bu[:], uap[:, kc, jj * 128:(jj + 1) * 128], hT[:, kc, tsl], kc == 0, kc == KC - 1,
                                   [("WB", si), ("hT", kc, blk)], [bun])
                            act(SIL[bi], bg[:], AF.Silu, [bgn], [("sil", bi)])
                            tt("dve", actT[:, j, b2 * 512:(b2 + 1) * 512], SIL[bi], bu[:], ALU.mult, [("sil", bi), bun],
                               [("actT", j, b2)] + mixall)
                for fp in range(4):
                    si = wslot()
                    dap = WB[si][:, 0:NJ * 256].rearrange("p (j n) -> p j n", j=NJ)
                    dma("pool", dap, dv[:, :, fp * 256:(fp + 1) * 256], "wb%d" % si, (), [("WB", si)], nobar=True)
                    for f2 in range(2):
                        fc = fp * 2 + f2
                        for b2 in range(2):
                            blk = half * 2 + b2
                            tsl = slice(blk * 512, (blk + 1) * 512)
                            bi = 4 + (cnt % 2)
                            cnt += 1
                            bank, bname = PS[bi], "ps%d" % bi
                            for j in range(NJ):
                                mm(bank[:], dap[:, j, f2 * 128:(f2 + 1) * 128], actT[:, j, b2 * 512:(b2 + 1) * 512],
                                   j == 0, j == NJ - 1, [("WB", si), ("actT", j, b2)], [bname])
                            tt("dve", xT[:, fc, tsl], xT[:, fc, tsl], bank[:], ALU.add, [bname, ("xT", fc, blk)],
                               [("xT", fc, blk)])

        mixall = [("mixT", kc, blk) for kc in range(KC) for blk in range(4)]

        def final_out():
            gbc = SCRF[:, 3072:4096]
            junk = [SCRF[:, 4096 + i * 512:4096 + (i + 1) * 512] for i in range(2)]
            ot = [SCRF[:, 5120 + i * 1024:5120 + (i + 1) * 1024] for i in range(4)]
            ssall = SCRF[:, 12288:12304]
            dma("sp", gbc, fng_d[0].partition_broadcast(128), "c3", (), ["gbc"])
            fin = []
            for tile in range(16):
                o = tile % 4
                ss = ssall[:, o * 4:o * 4 + 4]
                bp = 3 if tile < 8 else o
                banks = [(PS[bp * 2 + hf], "ps%d" % (bp * 2 + hf)) for hf in range(2)]
                for hf in range(2):
                    bank, bname = banks[hf]
                    for j in range(4):
                        kc = hf * 4 + j
                        tr(bank[:, j * 128:(j + 1) * 128], xT[:, kc, tile * 128:(tile + 1) * 128], ident_f,
                           [("xT", kc, tile // 4), "kf"], [bname])
                    act(junk[hf], bank[:], AF.Square, [bname], [("junk", hf), ("ss", o, hf)], accum_out=ss[:, hf:hf + 1])
                tt("dve", ss[:, 2:3], ss[:, 0:1], ss[:, 1:2], ALU.add, [("ss", o, 0), ("ss", o, 1)], [("ss", o, 2)])
                act(ss[:, 2:3], ss[:, 2:3], AF.Ln, [("ss", o, 2)], [("ss", o, 2)], scale=1.0 / D, bias=EPS)
                act(ss[:, 3:4], ss[:, 2:3], AF.Exp, [("ss", o, 2)], [("ss", o, 3)], scale=-0.5)
                for hf in range(2):
                    bank, bname = banks[hf]
                    stt(ot[o][:, hf * 512:(hf + 1) * 512], bank[:], ss[:, 3:4], gbc[:, hf * 512:(hf + 1) * 512],
                        ALU.mult, ALU.mult, [bname, ("ss", o, 3), "gbc"], [("ot", o)])
                fin.append(dma("sp", out_d[tile * 128:(tile + 1) * 128, :], ot[o], "out%d" % o, [("ot", o)], []))
            return fin

        final_ops = []
        if stage in ("full", "mix", "l0", "hg"):
            lower_bounds()
        nlayers = DEPTH if stage == "full" else 1
        if stage in ("conv", "na", "hg"):
            P.op("pool", lambda e: e.memset(BIG[:, 0:KC * T], 0.0), (), mixall + pages(BIG[:, 0:KC * T]))
        for layer in range(nlayers):
            rmsnorm(C_MIXG + layer * KC, "m%d" % layer)
            if stage == "norm":
                break
            if stage in ("full", "mix", "l0", "conv"):
                conv_mixer(layer)
            if stage in ("full", "mix", "l0", "na"):
                na_mixer(layer)
            if stage in ("full", "mix", "l0", "hg"):
                for h in range(4):
                    hgrn2_head(layer, h)
            if stage in ("mix", "conv", "na", "hg"):
                break
            out_proj(layer)
            rmsnorm(C_FFNG + layer * KC, "f%d" % layer)
            ffn(layer)
        if dbg is not None:
            name = dbg[0]
            if name == "hT":
                src = hT[:].rearrange("p k t -> p (k t)")
                rd = [("hT", kc, b) for kc in range(KC) for b in range(4)]
            elif name == "mixT":
                src = mixT.rearrange("p k t -> p (k t)")
                rd = mixall
            elif name == "xT":
                src = xT[:].rearrange("p k t -> p (k t)")
                rd = [("xT", kc, b) for kc in range(KC) for b in range(4)]
            final_ops.append(dma("sp", dbg_d, src, "dbg", rd, []))
        if stage in ("full", "l0"):
            final_ops += final_out()
        P.emit(final_ops)
    return nc


def _consts():
    kfc = np.zeros((128, NKF), np.float32)
    s = np.arange(128)[:, None]
    c = np.arange(128)[None, :]
    same = (s // HCH) == (c // HCH)
    kfc[:, K_IDF:K_IDF + 128] = np.eye(128, dtype=np.float32)
    _unused_K_UF = (same & (s <= c))
    _unused_K_UB = (same & (s >= c))
    _unused_K_MF = (same & (s > c))
    _unused_K_MB = (same & (s < c))
    kfc[:, K_SCM:K_SCM + 512] = ((np.arange(512) % HCH) != 0)[None, :]
    kbc = np.zeros((128, NKB), np.float32)
    kbc[:, B_ID:B_ID + 128] = np.eye(128, dtype=np.float32)
    kbc[:, B_ONES:B_ONES + 128] = 1.0
    kbc[:, B_OBD:B_OBD + 128] = ((s // 64) == (c // 64))
    kbc[:, B_UF:B_UF + 128] = (same & (s <= c))
    kbc[:, B_UB:B_UB + 128] = (same & (s >= c))
    kbc[:, B_NUF:B_NUF + 128] = -1.0 * (same & (s <= c))
    kbc[:, B_NUB:B_NUB + 128] = -1.0 * (same & (s >= c))
    kcol = np.arange(64)[:, None]
    qcol = np.arange(64)[None, :]
    qs = np.clip(qcol - 8, 0, 48)
    valid = (kcol >= qs) & (kcol < qs + 16)
    colneg = np.where(valid, 0.0, NEGM).astype(np.float32)
    colneg = np.concatenate([colneg, colneg], axis=0)
    return kfc, kbc, colneg


def _layout_small(inp):
    cols = np.zeros((128, NCOLS), np.float32)
    for l in range(DEPTH):
        cols[:, C_MIXG + l * 8:C_MIXG + (l + 1) * 8] = inp["mix_norm_g"][l].reshape(8, 128).T
        cols[:, C_FFNG + l * 8:C_FFNG + (l + 1) * 8] = inp["ffn_norm_g"][l].reshape(8, 128).T
        cols[:, C_HGN + l] = inp["hg_norm_g"][l]
        for c in range(2):
            cols[:, C_CONVW + (l * 2 + c) * 31:C_CONVW + (l * 2 + c + 1) * 31] = inp["conv_w"][l][:, c * 128:(c + 1) * 128].T
            cols[:, C_CONVB + l * 2 + c] = inp["conv_b"][l][c * 128:(c + 1) * 128]
            cols[:, C_LNG + l * 2 + c] = inp["conv_ln_g"][l][c * 128:(c + 1) * 128]
            cols[:, C_LNB + l * 2 + c] = inp["conv_ln_b"][l][c * 128:(c + 1) * 128]
    rpb = inp["na_rpb"]
    kcol = np.arange(64)[:, None]
    qcol = np.arange(64)[None, :]
    ci = kcol - qcol + 15
    ok = (ci >= 0) & (ci <= 30)
    cic = np.clip(ci, 0, 30)
    nab = np.zeros((DEPTH, 2, 64, 2, 15, 64), np.float32)
    for l in range(DEPTH):
        for pr in range(2):
            for a in range(2):
                for t in range(15):
                    g = rpb[l, 2 * pr + a, 14 - t][cic]
                    nab[l, a, :, pr, t, :] = np.where(ok, g, np.float32(0.0))
    nab = nab.reshape(DEPTH, 128, 2 * 15 * 64)
    lbraw = np.ascontiguousarray(inp["hg_lower_bounds"].reshape(DEPTH * 2 * 4, 128).T).astype(np.float32)
    fng = np.ascontiguousarray(inp["final_norm_g"].reshape(1, D)).astype(np.float32)
    return cols, nab, lbraw, fng


_NC_CACHE = {}


def _get_nc(stage="full", dbg=None):
    key = (stage, dbg)
    if key not in _NC_CACHE:
        _NC_CACHE[key] = build_program(stage, dbg)
    return _NC_CACHE[key]


def make_in_maps(inp, ncores):
    kfc, kbc, colneg = _consts()
    cols, nab, lbraw, fng = _layout_small(inp)
    f = lambda a: np.ascontiguousarray(np.asarray(a, dtype=np.float32))
    shared = dict(w_in=f(inp["w_in"]), w_out=f(inp["w_out"]), w_gate_up=f(inp["w_gate_up"]), w_down=f(inp["w_down"]),
                  cols=cols, kf=kfc, kb=kbc, lbraw=lbraw, fng=fng, nab=nab, colneg=colneg)
    x = f(inp["x"])
    return [dict(shared, x=x[i]) for i in range(ncores)]


def kernel(**inputs):
    nc = _get_nc("full", None)
    in_maps = make_in_maps(inputs, 8)
    res = run_bass_kernel_spmd(nc, in_maps, core_ids=list(range(8)))
    return np.stack([np.asarray(r["out"], dtype=np.float32) for r in res.results], axis=0)
```
